# Optimizing a Trainium2 kernel written in Bass

```python
import jax, jax.numpy as jnp
from jax import lax
import numpy as np

D_MODEL = 2048
BATCH = 1
SEQ = 8192
DEPTH = 4

N_MIXERS = 2
N_ATTN_LAYERS = (DEPTH + N_MIXERS - 1) // N_MIXERS
N_RNN_LAYERS = DEPTH // N_MIXERS

GRID_W = 64
ROPE_THETA = 10000.0

HEAD_DIM = 128
N_Q_HEADS = D_MODEL // HEAD_DIM
N_KV_HEADS = 4
GQA_GROUP = N_Q_HEADS // N_KV_HEADS
Q_BLOCK = 128
ROPE_AXIS_DIM = HEAD_DIM // 2
ROPE_FREQS = ROPE_AXIS_DIM // 2
QKV_DIM = (N_Q_HEADS + 2 * N_KV_HEADS) * HEAD_DIM

D_RNN = D_MODEL
RNN_BLOCK_W = 256
RNN_BLOCKS = D_RNN // RNN_BLOCK_W
CONV_W = 4
CONV_LEFT = 2
CONV_RIGHT = CONV_W - 1 - CONV_LEFT
LRU_C = 8.0

D_FF = -(-8 * D_MODEL // (3 * 256)) * 256

EPS = 1e-6

kernel_name = "hybrid_axial_gqa_rglru_encoder"


def rms_norm(x, g):
    xf = x.astype(jnp.float32)
    y = xf * lax.rsqrt(jnp.mean(xf * xf, axis=-1, keepdims=True) + EPS)
    return (y * g.astype(jnp.float32)).astype(x.dtype)


def axial_rope_tables(seq_len):
    rows_n = seq_len // GRID_W
    freqs = ROPE_THETA ** (-jnp.arange(ROPE_FREQS, dtype=jnp.float32) / ROPE_FREQS)
    row_ang = jnp.arange(rows_n, dtype=jnp.float32)[:, None, None] * freqs
    col_ang = jnp.arange(GRID_W, dtype=jnp.float32)[None, :, None] * freqs
    shape = (rows_n, GRID_W, ROPE_FREQS)
    ang = jnp.stack([jnp.broadcast_to(row_ang, shape), jnp.broadcast_to(col_ang, shape)], axis=-2)
    ang = ang.reshape(seq_len, 2, ROPE_FREQS)
    return jnp.cos(ang), jnp.sin(ang)


def apply_axial_rope(x, cos, sin):
    xr = x.astype(jnp.float32).reshape(*x.shape[:-1], 2, 2, ROPE_FREQS)
    x1, x2 = xr[..., 0, :], xr[..., 1, :]
    c = cos[None, :, None]
    s = sin[None, :, None]
    out = jnp.stack([x1 * c - x2 * s, x2 * c + x1 * s], axis=-2)
    return out.reshape(x.shape).astype(x.dtype)


def attention_mixer(h, w_qkv, q_gain, k_gain, w_o, cos, sin):
    b, s, _ = h.shape
    qkv = h @ w_qkv
    q, k, v = jnp.split(qkv, [N_Q_HEADS * HEAD_DIM, (N_Q_HEADS + N_KV_HEADS) * HEAD_DIM], axis=-1)
    q = q.reshape(b, s, N_Q_HEADS, HEAD_DIM)
    k = k.reshape(b, s, N_KV_HEADS, HEAD_DIM)
    v = v.reshape(b, s, N_KV_HEADS, HEAD_DIM)
    q = apply_axial_rope(rms_norm(q, q_gain), cos, sin) * (HEAD_DIM ** -0.5)
    k = apply_axial_rope(rms_norm(k, k_gain), cos, sin)
    n_blocks = s // Q_BLOCK
    qb = q.reshape(b, n_blocks, Q_BLOCK, N_KV_HEADS, GQA_GROUP, HEAD_DIM).transpose(1, 0, 2, 3, 4, 5)

    def one_block(q_blk):
        sc = jnp.einsum('bqkgd,bskd->bkgqs', q_blk, k).astype(jnp.float32)
        p = jax.nn.softmax(sc, axis=-1).astype(v.dtype)
        return jnp.einsum('bkgqs,bskd->bqkgd', p, v)

    o = lax.map(one_block, qb)
    o = o.transpose(1, 0, 2, 3, 4, 5).reshape(b, s, N_Q_HEADS * HEAD_DIM)
    return o @ w_o


def depthwise_conv_centred(x, w, bias):
    y = lax.conv_general_dilated(
        x, w[:, None, :], window_strides=(1,), padding=[(CONV_LEFT, CONV_RIGHT)],
        dimension_numbers=('NWC', 'WIO', 'NWC'), feature_group_count=x.shape[-1])
    return y + bias


def block_diag_linear(x, w, bias):
    b, s, _ = x.shape
    xb = x.reshape(b, s, RNN_BLOCKS, RNN_BLOCK_W)
    return jnp.einsum('bsni,nij->bsnj', xb, w).reshape(b, s, D_RNN) + bias


def rg_lru(x, w_a, b_a, w_i, b_i, lam, reverse):
    xf = x.astype(jnp.float32)
    r = jax.nn.sigmoid(block_diag_linear(xf, w_a, b_a).astype(jnp.float32))
    i = jax.nn.sigmoid(block_diag_linear(xf, w_i, b_i).astype(jnp.float32))
    log_a = -LRU_C * r * jax.nn.softplus(-lam.astype(jnp.float32))
    a = jnp.exp(log_a)
    u = jnp.sqrt(-jnp.expm1(2.0 * log_a)) * (i * xf)

    def combine(e1, e2):
        a1, b1 = e1
        a2, b2 = e2
        return a1 * a2, a2 * b1 + b2

    _, hs = lax.associative_scan(combine, (a, u), axis=1, reverse=reverse)
    return hs


def recurrent_mixer(h, w_in, conv_w, conv_b, w_a, b_a, w_i, b_i, lam, w_out):
    xy = h @ w_in
    xb, yb = jnp.split(xy, 2, axis=-1)
    yb = jax.nn.gelu(yb, approximate=True)
    xb = depthwise_conv_centred(xb, conv_w, conv_b)
    h_fwd = rg_lru(xb, w_a[0], b_a[0], w_i[0], b_i[0], lam[0], reverse=False)
    h_bwd = rg_lru(xb, w_a[1], b_a[1], w_i[1], b_i[1], lam[1], reverse=True)
    out = (h_fwd + h_bwd).astype(h.dtype) * yb
    return out @ w_out


def swiglu(h, w_gate, w_up, w_down):
    return (jax.nn.silu(h @ w_gate) * (h @ w_up)) @ w_down


def setup_inputs(seed: int = 0) -> dict:
    key = jax.random.key(seed)
    ks = jax.random.split(key, 24)
    f32 = jnp.float32

    def nrm(k, shape, fan_in):
        return jax.random.normal(k, shape, f32) * (fan_in ** -0.5)

    def small(k, shape, scale=0.01):
        return jax.random.normal(k, shape, f32) * scale

    u = jax.random.uniform(ks[15], (N_RNN_LAYERS, 2, D_RNN), f32, minval=0.9, maxval=0.999)
    s_lam = u ** (1.0 / LRU_C)
    rnn_lambda = jnp.log(s_lam) - jnp.log1p(-s_lam)

    return {
        "x": jax.random.normal(ks[0], (BATCH, SEQ, D_MODEL), f32),
        "norm_mix": 1.0 + small(ks[1], (DEPTH, D_MODEL), 0.02),
        "norm_ffn": 1.0 + small(ks[2], (DEPTH, D_MODEL), 0.02),
        "attn_w_qkv": nrm(ks[3], (N_ATTN_LAYERS, D_MODEL, QKV_DIM), D_MODEL),
        "attn_q_gain": 1.0 + small(ks[4], (N_ATTN_LAYERS, HEAD_DIM), 0.02),
        "attn_k_gain": 1.0 + small(ks[5], (N_ATTN_LAYERS, HEAD_DIM), 0.02),
        "attn_w_o": nrm(ks[6], (N_ATTN_LAYERS, N_Q_HEADS * HEAD_DIM, D_MODEL), N_Q_HEADS * HEAD_DIM),
        "rnn_w_in": nrm(ks[7], (N_RNN_LAYERS, D_MODEL, 2 * D_RNN), D_MODEL),
        "rnn_conv_w": nrm(ks[8], (N_RNN_LAYERS, CONV_W, D_RNN), CONV_W),
        "rnn_conv_b": small(ks[9], (N_RNN_LAYERS, D_RNN)),
        "rnn_w_a": nrm(ks[10], (N_RNN_LAYERS, 2, RNN_BLOCKS, RNN_BLOCK_W, RNN_BLOCK_W), RNN_BLOCK_W),
        "rnn_b_a": small(ks[11], (N_RNN_LAYERS, 2, D_RNN)),
        "rnn_w_i": nrm(ks[12], (N_RNN_LAYERS, 2, RNN_BLOCKS, RNN_BLOCK_W, RNN_BLOCK_W), RNN_BLOCK_W),
        "rnn_b_i": small(ks[13], (N_RNN_LAYERS, 2, D_RNN)),
        "rnn_lambda": rnn_lambda,
        "rnn_w_out": nrm(ks[14], (N_RNN_LAYERS, D_RNN, D_MODEL), D_RNN),
        "ffn_w_gate": nrm(ks[16], (DEPTH, D_MODEL, D_FF), D_MODEL),
        "ffn_w_up": nrm(ks[17], (DEPTH, D_MODEL, D_FF), D_MODEL),
        "ffn_w_down": nrm(ks[18], (DEPTH, D_FF, D_MODEL), D_FF),
    }


def reference(x, norm_mix, norm_ffn, attn_w_qkv, attn_q_gain, attn_k_gain, attn_w_o,
              rnn_w_in, rnn_conv_w, rnn_conv_b, rnn_w_a, rnn_b_a, rnn_w_i, rnn_b_i,
              rnn_lambda, rnn_w_out, ffn_w_gate, ffn_w_up, ffn_w_down):
    seq_len = x.shape[1]
    cos, sin = axial_rope_tables(seq_len)
    for i in range(DEPTH):
        j = i // N_MIXERS
        h = rms_norm(x, norm_mix[i])
        if i % N_MIXERS == 0:
            mix = attention_mixer(h, attn_w_qkv[j], attn_q_gain[j], attn_k_gain[j], attn_w_o[j], cos, sin)
        else:
            mix = recurrent_mixer(h, rnn_w_in[j], rnn_conv_w[j], rnn_conv_b[j], rnn_w_a[j], rnn_b_a[j],
                                  rnn_w_i[j], rnn_b_i[j], rnn_lambda[j], rnn_w_out[j])
        x = x + mix
        x = x + swiglu(rms_norm(x, norm_ffn[i]), ffn_w_gate[i], ffn_w_up[i], ffn_w_down[i])
    return x
```

```python
import contextlib
import numpy as np
import concourse.bass as bass
import concourse.mybir as mybir
from concourse.bass_utils import run_bass_kernel_spmd

F32 = mybir.dt.float32
BF16 = mybir.dt.bfloat16
AF = mybir.ActivationFunctionType
ALU = mybir.AluOpType

NCORES = 8
D = 2048
S = 8192
TOK = S // NCORES
KC = D // 128
DFF = 5632
FC = DFF // 128
NH = 16
NKV = 4
HD = 128
QKV = (NH + 2 * NKV) * HD
EPS = 1e-6
GRID_W = 64
BLK = 512
NBLK = TOK // BLK
LRU_C = 8.0


class Tok:
    __slots__ = ("eng", "sem", "val")

    def __init__(self, eng, sem, val):
        self.eng, self.sem, self.val = eng, sem, val


class EngS:
    def __init__(self, name, eng, sem):
        self.name, self.eng, self.sem = name, eng, sem
        self.count = 0
        self.pending = []
        self.waited = {}
        self.dma_sems = []
        self.dma_vals = []
        self.dma_i = 0


class Res:
    __slots__ = ("w", "r")

    def __init__(self):
        self.w = None
        self.r = {}


class K:
    def __init__(self, nc, n_dma_sems=20):
        self.nc = nc
        self.es = contextlib.ExitStack()
        self.res = {}
        self.engs = {}
        for name, eng in (("pe", nc.tensor), ("act", nc.scalar), ("dve", nc.vector),
                          ("pool", nc.gpsimd), ("sp", nc.sync)):
            sem = self.es.enter_context(nc.semaphore("s_" + name))
            self.engs[name] = EngS(name, eng, sem)
        for name in ("pool", "sp"):
            E = self.engs[name]
            for i in range(n_dma_sems):
                E.dma_sems.append(self.es.enter_context(nc.semaphore(f"d_{name}{i}")))
                E.dma_vals.append(0)
        self.n_wait = 0
        self.log = []
        self.n_ins = 0

    def sb(self, name, shape, dt):
        return self.es.enter_context(self.nc.sbuf_tensor("sb_" + name, list(shape), dt))

    def ps(self, name, shape, dt):
        return self.es.enter_context(self.nc.psum_tensor(name, list(shape), dt))

    def _wait(self, E, tok):
        if tok.val is None:
            raise RuntimeError("dependency on an instruction without milestone (engine %s)" % tok.eng)
        key = id(tok.sem)
        if E.waited.get(key, 0) >= tok.val:
            return
        E.eng.wait_ge(tok.sem, tok.val)
        self.log.append((E.name, 'wait', tok.eng, tok.val))
        E.waited[key] = tok.val
        self.n_wait += 1

    def op(self, engname, emit, reads=(), writes=(), inc=True, dma=False):
        E = self.engs[engname]
        deps = []
        for r in reads:
            st = self.res.get(r)
            if st is not None and st.w is not None:
                deps.append(st.w)
        for w in writes:
            st = self.res.get(w)
            if st is not None:
                if st.w is not None:
                    deps.append(st.w)
                deps.extend(st.r.values())
        for tok in deps:
            if engname == "pe" and tok.eng == "pe":
                continue
            self._wait(E, tok)
        if dma:
            i = E.dma_i % len(E.dma_sems)
            E.dma_i += 1
            sem = E.dma_sems[i]
            if E.dma_vals[i] > 0:
                self._wait(E, Tok("dma", sem, E.dma_vals[i]))
            ins = emit()
            E.dma_vals[i] += 16
            ins.then_inc(sem, 16)
            tok = Tok("dma%d" % id(sem), sem, E.dma_vals[i])
        else:
            ins = emit()
            if inc:
                E.count += 1
                ins.then_inc(E.sem, 1)
                tok = Tok(engname, E.sem, E.count)
                for p in E.pending:
                    p.val = E.count
                E.pending = []
            else:
                tok = Tok(engname, E.sem, None)
                E.pending.append(tok)
        self.n_ins += 1
        self.log.append((engname, 'op', tuple(writes), tok.val))
        for w in writes:
            st = self.res.setdefault(w, Res())
            st.w = tok
            st.r = {}
        for r in reads:
            st = self.res.setdefault(r, Res())
            st.r[tok.eng] = tok
        return tok

    def finish(self, final_resources):
        E = self.engs["sp"]
        for r in final_resources:
            st = self.res.get(r)
            if st is not None and st.w is not None:
                self._wait(E, st.w)

    def close(self):
        self.es.close()


class Prog:
    def __init__(self, nc):
        self.nc = nc
        self.k = K(nc)
        k = self.k
        self.ones_bf = k.sb("ones_bf", [128, 128], BF16)
        self.ones_f = k.sb("ones_f", [128, 128], F32)
        self.eps_t = k.sb("eps_t", [128, 1], F32)
        k.op("dve", lambda: nc.vector.memset(self.ones_bf[:], 1.0), writes=["ones_bf"])
        k.op("dve", lambda: nc.vector.memset(self.ones_f[:], 1.0), writes=["ones_f"])
        k.op("dve", lambda: nc.vector.memset(self.eps_t[:], EPS), writes=["eps_t"])
        self.psall = k.ps("psall", [128, 8, 512], F32)
        self.psb = [self.psall[:, i, :] for i in range(8)]
        self.rr = {}

    def ring(self, name, n):
        i = self.rr.get(name, 0)
        self.rr[name] = i + 1
        return i % n

    def dma(self, q, out, in_, reads=(), writes=()):
        eng = self.nc.gpsimd if q == "pool" else self.nc.sync
        return self.k.op(q, lambda: eng.dma_start(out=out, in_=in_), reads=reads, writes=writes, dma=True)

    def alloc_x(self):
        self.xT = self.k.sb("xT", [128, KC, TOK], F32)

    def load_x(self, x_dram):
        src = x_dram.rearrange("(k p) t -> p k t", p=128)
        for kc in range(KC):
            self.dma("sp", self.xT[:, kc, :], src[:, kc, :],
                     writes=[("x", kc, b) for b in range(NBLK)])

    def store_x(self, y_dram):
        dst = y_dram.rearrange("(k p) t -> p k t", p=128)
        for kc in range(KC):
            self.dma("sp", dst[:, kc, :], self.xT[:, kc, :],
                     reads=[("x", kc, b) for b in range(NBLK)], writes=[("yout", kc)])
        self.k.finish([("yout", kc) for kc in range(KC)])

    def load_vec(self, name, dram_1d, n):
        t = self.k.sb(name, [128, n], F32)
        with self.nc.allow_non_contiguous_dma(reason="tiny per-feature vector"):
            self.dma("sp", t[:], dram_1d.rearrange("(c p) -> p c", p=128), writes=[name])
        return t

    def alloc_norm(self):
        k = self.k
        self.hT = k.sb("hT", [128, KC, BLK], BF16)
        self.sq = [k.sb(f"sq{i}", [128, BLK], BF16) for i in range(3)]
        self.rt = k.sb("rt", [128, BLK], F32)
        self.rstd = k.sb("rstd", [128, BLK], F32)

    def rmsnorm(self, b, gain_t, gain_name, ps_i=7):
        nc, k = self.nc, self.k
        ps = self.psb[ps_i]
        bs = slice(b * BLK, (b + 1) * BLK)
        for kc in range(KC):
            i = self.ring("sq", 3)
            k.op("act", lambda: nc.scalar.activation(out=self.sq[i][:], in_=self.xT[:, kc, bs], func=AF.Square),
                 reads=[("x", kc, b)], writes=[("sq", i)])
            k.op("pe", lambda: nc.tensor.matmul(ps, lhsT=self.ones_bf[:], rhs=self.sq[i][:],
                                                start=(kc == 0), stop=(kc == KC - 1)),
                 reads=[("sq", i), "ones_bf"], writes=[("ps", ps_i)])
        k.op("act", lambda: nc.scalar.activation(out=self.rt[:], in_=ps, func=AF.Sqrt,
                                                 bias=self.eps_t[:], scale=1.0 / D),
             reads=[("ps", ps_i), "eps_t"], writes=["rt"])
        k.op("dve", lambda: nc.vector.reciprocal(out=self.rstd[:], in_=self.rt[:]),
             reads=["rt"], writes=["rstd"])
        for kc in range(KC):
            k.op("dve", lambda: nc.vector.scalar_tensor_tensor(
                out=self.hT[:, kc, :], in0=self.xT[:, kc, bs], scalar=gain_t[:, kc:kc + 1],
                in1=self.rstd[:], op0=ALU.mult, op1=ALU.mult),
                 reads=[("x", kc, b), "rstd", gain_name], writes=[("h", kc)])

    WSLOT = FC * 256

    def alloc_wring(self, n=2):
        self.nw = n
        self.wbuf = [self.k.sb(f"wbuf{i}", [128, self.WSLOT], BF16) for i in range(n)]

    def wslot(self):
        return self.ring("w", self.nw)

    def alloc_ffn(self):
        k = self.k
        self.aT = k.sb("aT", [128, FC, BLK], BF16)
        self.sg = [k.sb(f"sg{i}", [128, BLK], F32) for i in range(2)]

    def ffn(self, b, wg, wu, wd):
        nc, k = self.nc, self.k
        bs = slice(b * BLK, (b + 1) * BLK)
        wg_v = wg.rearrange("(k p) n -> p k n", p=128)
        wu_v = wu.rearrange("(k p) n -> p k n", p=128)
        wd_v = wd.rearrange("(c p) n -> p c n", p=128)
        CW = 256
        for pr in range(DFF // CW):
            s = self.wslot()
            wgb = self.wbuf[s][:, 0:KC * CW].rearrange("p (k n) -> p k n", k=KC)
            wub = self.wbuf[s][:, KC * CW:2 * KC * CW].rearrange("p (k n) -> p k n", k=KC)
            self.dma("pool", wgb, wg_v[:, :, pr * CW:(pr + 1) * CW], writes=[("w", s, 0)])
            self.dma("pool", wub, wu_v[:, :, pr * CW:(pr + 1) * CW], writes=[("w", s, 1)])
            for c in range(CW // 128):
                cc = pr * (CW // 128) + c
                j = self.ring("gu", 2)
                pg, pu = self.psb[j], self.psb[2 + j]
                for kc in range(KC):
                    k.op("pe", lambda: nc.tensor.matmul(pg, lhsT=wgb[:, kc, c * 128:(c + 1) * 128],
                                                        rhs=self.hT[:, kc, :], start=(kc == 0), stop=(kc == KC - 1)),
                         reads=[("w", s, 0), ("h", kc)], writes=[("ps", j)], inc=(kc == KC - 1))
                for kc in range(KC):
                    k.op("pe", lambda: nc.tensor.matmul(pu, lhsT=wub[:, kc, c * 128:(c + 1) * 128],
                                                        rhs=self.hT[:, kc, :], start=(kc == 0), stop=(kc == KC - 1)),
                         reads=[("w", s, 1), ("h", kc)], writes=[("ps", 2 + j)], inc=(kc == KC - 1))
                g = self.ring("sg", 2)
                k.op("act", lambda: nc.scalar.activation(out=self.sg[g][:], in_=pg, func=AF.Silu),
                     reads=[("ps", j)], writes=[("sg", g)])
                k.op("dve", lambda: nc.vector.tensor_tensor(out=self.aT[:, cc, :], in0=pu, in1=self.sg[g][:],
                                                            op=ALU.mult),
                     reads=[("ps", 2 + j), ("sg", g)], writes=[("a", cc)])
        for pr in range(D // CW):
            s = self.wslot()
            wdb = self.wbuf[s][:, 0:FC * CW].rearrange("p (c n) -> p c n", c=FC)
            self.dma("pool", wdb, wd_v[:, :, pr * CW:(pr + 1) * CW], writes=[("w", s, 0), ("w", s, 1)])
            for c in range(CW // 128):
                jj = pr * (CW // 128) + c
                j = 4 + self.ring("dn", 2)
                pd = self.psb[j]
                for fc in range(FC):
                    k.op("pe", lambda: nc.tensor.matmul(pd, lhsT=wdb[:, fc, c * 128:(c + 1) * 128],
                                                        rhs=self.aT[:, fc, :], start=(fc == 0), stop=(fc == FC - 1)),
                         reads=[("w", s, 0), ("a", fc)], writes=[("ps", j)], inc=(fc == FC - 1))
                k.op("dve", lambda: nc.vector.tensor_tensor(out=self.xT[:, jj, bs], in0=pd, in1=self.xT[:, jj, bs],
                                                            op=ALU.add),
                     reads=[("ps", j), ("x", jj, b)], writes=[("x", jj, b)])


    def alloc_proj(self):
        self.pw = [self.k.sb(f"pw{i}", [128, KC, 256], BF16) for i in range(2)]

    def proj_resid(self, inT, in_res, w):
        nc, k = self.nc, self.k
        w_v = w.rearrange("(k p) n -> p k n", p=128)
        CW = 256
        for pr in range(D // CW):
            s = self.ring("pw", 2)
            self.dma("pool", self.pw[s][:], w_v[:, :, pr * CW:(pr + 1) * CW], writes=[("pw", s)])
            for c in range(CW // 128):
                jj = pr * (CW // 128) + c
                for b in range(NBLK):
                    bs = slice(b * BLK, (b + 1) * BLK)
                    j = 4 + self.ring("dn", 2)
                    pd = self.psb[j]
                    for kc in range(KC):
                        k.op("pe", lambda: nc.tensor.matmul(pd, lhsT=self.pw[s][:, kc, c * 128:(c + 1) * 128],
                                                            rhs=inT[:, kc, bs], start=(kc == 0), stop=(kc == KC - 1)),
                             reads=[("pw", s), in_res(kc)], writes=[("ps", j)], inc=(kc == KC - 1))
                    k.op("dve", lambda: nc.vector.tensor_tensor(out=self.xT[:, jj, bs], in0=pd, in1=self.xT[:, jj, bs],
                                                                op=ALU.add),
                         reads=[("ps", j), ("x", jj, b)], writes=[("x", jj, b)])

    def alloc_attn_pre(self):
        k = self.k
        self.qT = k.sb("qT", [128, NH, TOK], BF16)
        self.kTl = k.sb("kTl", [128, NKV, TOK], BF16)
        self.vl = k.sb("vl", [128, TOK // 128, NKV * HD], BF16)
        self.cosT = k.sb("cosT", [128, TOK], F32)
        self.sinT = k.sb("sinT", [128, TOK], F32)
        self.rm = k.sb("rm", [128, 128], F32)
        self.sqf = [k.sb(f"sqf{i}", [128, BLK], F32) for i in range(2)]
        self.rt2 = k.sb("rt2", [128, BLK], F32)
        self.rs2 = k.sb("rs2", [128, BLK], F32)
        self.qn = [k.sb(f"qn{i}", [128, BLK], F32) for i in range(2)]
        self.t1 = k.sb("t1", [128, BLK], F32)
        self.t2 = k.sb("t2", [128, BLK], F32)
        self.wq = [k.sb(f"wq{i}", [128, KC, 512], BF16) for i in range(2)]

    def attn_pre(self, wqkv, g_mix, g_mix_name, qg, kg, kT_out, v_out):
        nc, k = self.nc, self.k
        w_v = wqkv.rearrange("(k p) n -> p k n", p=128)
        for b in range(NBLK):
            bs = slice(b * BLK, (b + 1) * BLK)
            self.rmsnorm(b, g_mix, g_mix_name)
            stA, stB, stC = {}, {}, {}

            def A(c, ws):
                j = c % 2
                pq = self.psb[j]
                for kc in range(KC):
                    k.op("pe", lambda: nc.tensor.matmul(pq, lhsT=self.wq[ws][:, kc, (c % 4) * 128:(c % 4 + 1) * 128],
                                                        rhs=self.hT[:, kc, :], start=(kc == 0), stop=(kc == KC - 1)),
                         reads=[("wq", ws), ("h", kc)], writes=[("ps", j)], inc=(kc == KC - 1))
                k.op("act", lambda: nc.scalar.activation(out=self.sqf[j][:], in_=pq, func=AF.Square),
                     reads=[("ps", j)], writes=[("sqf", j)])

            def B(c):
                j = c % 2
                pq = self.psb[j]
                gt, gname = (qg, "qg") if c < NH else (kg, "kg")
                k.op("pe", lambda: nc.tensor.matmul(self.psb[2], lhsT=self.ones_f[:], rhs=self.sqf[j][:], start=True, stop=True),
                     reads=[("sqf", j), "ones_f"], writes=[("ps", 2)])
                k.op("act", lambda: nc.scalar.activation(out=self.rt2[:], in_=self.psb[2], func=AF.Sqrt,
                                                         bias=self.eps_t[:], scale=1.0 / HD),
                     reads=[("ps", 2), "eps_t"], writes=["rt2"])
                k.op("dve", lambda: nc.vector.reciprocal(out=self.rs2[:], in_=self.rt2[:]), reads=["rt2"], writes=["rs2"])
                k.op("dve", lambda: nc.vector.scalar_tensor_tensor(out=self.qn[j][:], in0=pq, scalar=gt[:, 0:1], in1=self.rs2[:],
                                                                   op0=ALU.mult, op1=ALU.mult),
                     reads=[("ps", j), "rs2", gname], writes=[("qn", j)])

            def C(c):
                j = c % 2
                k.op("pe", lambda: nc.tensor.matmul(self.psb[3], lhsT=self.rm[:], rhs=self.qn[j][:], start=True, stop=True),
                     reads=[("qn", j), "rm"], writes=[("ps", 3)])
                k.op("dve", lambda: nc.vector.tensor_tensor(out=self.t1[:], in0=self.qn[j][:], in1=self.cosT[:, bs], op=ALU.mult),
                     reads=[("qn", j), "cosT"], writes=["t1"])
                k.op("dve", lambda: nc.vector.tensor_tensor(out=self.t2[:], in0=self.psb[3], in1=self.sinT[:, bs], op=ALU.mult),
                     reads=[("ps", 3), "sinT"], writes=["t2"])
                if c < NH:
                    dst, wr = self.qT[:, c, bs], [("q", c)]
                else:
                    dst, wr = self.kTl[:, c - NH, bs], [("kTl", c - NH)]
                k.op("dve", lambda: nc.vector.tensor_tensor(out=dst, in0=self.t1[:], in1=self.t2[:], op=ALU.add),
                     reads=["t1", "t2"], writes=wr)

            ws = 0
            nqk = NH + NKV
            for c in range(nqk + 2):
                if c < nqk:
                    if c % 4 == 0:
                        ws = self.ring("wq", 2)
                        self.dma("pool", self.wq[ws][:], w_v[:, :, c * 128:c * 128 + 512], writes=[("wq", ws)])
                    A(c, ws)
                if 0 <= c - 1 < nqk:
                    B(c - 1)
                if 0 <= c - 2 < nqk:
                    C(c - 2)
            ws = self.ring("wq", 2)
            self.dma("pool", self.wq[ws][:], w_v[:, :, nqk * 128:nqk * 128 + 512], writes=[("wq", ws)])
            for tt in range(BLK // 128):
                tg = b * (BLK // 128) + tt
                j = 4 + self.ring("dn", 2)
                pv = self.psb[j]
                for kc in range(KC):
                    k.op("pe", lambda: nc.tensor.matmul(pv, lhsT=self.hT[:, kc, tt * 128:(tt + 1) * 128],
                                                        rhs=self.wq[ws][:, kc, :], start=(kc == 0), stop=(kc == KC - 1)),
                         reads=[("wq", ws), ("h", kc)], writes=[("ps", j)], inc=(kc == KC - 1))
                k.op("act", lambda: nc.scalar.copy(out=self.vl[:, tg, :], in_=pv), reads=[("ps", j)], writes=[("vl", tg)])
        for g in range(NKV):
            self.dma("sp", kT_out[g], self.kTl[:, g, :], reads=[("kTl", g)], writes=[("kT_out", g)])
        self.dma("sp", v_out.rearrange("(t p) n -> p t n", p=128), self.vl[:],
                 reads=[("vl", t) for t in range(TOK // 128)], writes=["v_out"])

    def alloc_attn_core(self):
        k = self.k
        self.ktb = [k.sb(f"ktb{i}", [128, S], BF16) for i in range(2)]
        self.vb = [k.sb(f"vb{i}", [128, S // 128, HD], BF16) for i in range(2)]
        self.pT = [k.sb(f"pT{i}", [128, TOK], BF16) for i in range(3)]
        self.rz = k.sb("rz", [128, TOK], F32)

    def attn_core(self, kT_all, v_all, dbg_heads=NH, dbg_nkt=S // 128):
        nc, k = self.nc, self.k
        NKT = dbg_nkt
        scale = float(HD) ** -0.5
        ps_s = [self.psall[:, 0:2, :], self.psall[:, 2:4, :]]
        ps_o = self.psall[:, 4:6, :]
        ps_z = self.psall[:, 6:8, :]
        steps = [(hh, kt) for hh in range(dbg_heads) for kt in range(NKT)]

        def load_kv(g):
            gb = g % 2
            self.dma("sp", self.ktb[gb][:].rearrange("p (r t) -> p r t", r=NCORES),
                     kT_all[:, g].rearrange("r p t -> p r t"), writes=[("kt", gb)])
            for r in range(NCORES):
                self.dma("sp", self.vb[gb][:, r * 8:(r + 1) * 8, :],
                         v_all[r, :, g * HD:(g + 1) * HD].rearrange("(t p) d -> p t d", p=128),
                         writes=[("v", gb, r)])

        def QK(i):
            hh, kt = steps[i]
            g = hh // 4
            sb_ = i % 2
            for qh in range(2):
                k.op("pe", lambda: nc.tensor.matmul(ps_s[sb_][:, qh, :], lhsT=self.ktb[g % 2][:, kt * 128:(kt + 1) * 128],
                                                    rhs=self.qT[:, hh, qh * 512:(qh + 1) * 512], start=True, stop=True),
                     reads=[("kt", g % 2), ("q", hh)], writes=[("ps", 2 * sb_), ("ps", 2 * sb_ + 1)], inc=(qh == 1))

        def EXP(i):
            sb_ = i % 2
            pb = i % 3
            k.op("act", lambda: nc.scalar.activation(out=self.pT[pb][:], in_=ps_s[sb_].rearrange("p a b -> p (a b)"),
                                                     func=AF.Exp, scale=scale),
                 reads=[("ps", 2 * sb_), ("ps", 2 * sb_ + 1)], writes=[("pT", pb)])

        def PV(i):
            hh, kt = steps[i]
            g = hh // 4
            pb = i % 3
            for qh in range(2):
                k.op("pe", lambda: nc.tensor.matmul(ps_o[:, qh, :], lhsT=self.vb[g % 2][:, kt, :],
                                                    rhs=self.pT[pb][:, qh * 512:(qh + 1) * 512], start=(kt == 0), stop=(kt == NKT - 1)),
                     reads=[("pT", pb), ("v", g % 2, kt // 8)], writes=[("ps", 4), ("ps", 5)], inc=False)
                k.op("pe", lambda: nc.tensor.matmul(ps_z[:, qh, :], lhsT=self.ones_bf[:],
                                                    rhs=self.pT[pb][:, qh * 512:(qh + 1) * 512], start=(kt == 0), stop=(kt == NKT - 1)),
                     reads=[("pT", pb), "ones_bf"], writes=[("ps", 6), ("ps", 7)], inc=(qh == 1))

        def FIN(hh):
            k.op("dve", lambda: nc.vector.reciprocal(out=self.rz[:], in_=ps_z.rearrange("p a b -> p (a b)")),
                 reads=[("ps", 6), ("ps", 7)], writes=["rz"])
            k.op("dve", lambda: nc.vector.tensor_tensor(out=self.qT[:, hh, :], in0=ps_o.rearrange("p a b -> p (a b)"),
                                                        in1=self.rz[:], op=ALU.mult),
                 reads=[("ps", 4), ("ps", 5), "rz"], writes=[("q", hh)])

        load_kv(0)
        QK(0)
        n = len(steps)
        for i in range(n):
            hh, kt = steps[i]
            if kt == 0 and hh % 4 == 0 and hh // 4 + 1 < NKV and hh + 4 < dbg_heads + 3:
                load_kv(hh // 4 + 1)
            if i + 1 < n:
                QK(i + 1)
            EXP(i)
            PV(i)
            if kt == NKT - 1:
                FIN(hh)


    def rnn_pre(self, w_in, g_mix, g_name, xy_out):
        nc, k = self.nc, self.k
        w_v = w_in.rearrange("(k p) n -> p k n", p=128)
        self.stg = [k.sb(f"stg{i}", [128, BLK], F32) for i in range(3)]
        for b in range(NBLK):
            bs = slice(b * BLK, (b + 1) * BLK)
            self.rmsnorm(b, g_mix, g_name)
            ws = 0
            for c in range(2 * KC):
                if c % 4 == 0:
                    ws = self.ring("wq", 2)
                    self.dma("pool", self.wq[ws][:], w_v[:, :, c * 128:c * 128 + 512], writes=[("wq", ws)])
                j = c % 2
                pq = self.psb[j]
                for kc in range(KC):
                    k.op("pe", lambda: nc.tensor.matmul(pq, lhsT=self.wq[ws][:, kc, (c % 4) * 128:(c % 4 + 1) * 128],
                                                        rhs=self.hT[:, kc, :], start=(kc == 0), stop=(kc == KC - 1)),
                         reads=[("wq", ws), ("h", kc)], writes=[("ps", j)], inc=(kc == KC - 1))
                sg = self.ring("stg", 3)
                fn = AF.Copy if c < KC else AF.Gelu_apprx_tanh
                k.op("act", lambda: nc.scalar.activation(out=self.stg[sg][:], in_=pq, func=fn),
                     reads=[("ps", j)], writes=[("stg", sg)])
                self.dma("sp", xy_out[c * 128:(c + 1) * 128, bs], self.stg[sg][:], reads=[("stg", sg)], writes=[("xy_out", c, b)])
        k.finish([("xy_out", c, b) for c in range(2 * KC) for b in range(NBLK)])

    def rnn_core(self, xb_d, yb_d, cw_d, cb_d, wa_d, ba_d, wi_d, bi_d, lam_d, out_d):
        nc, k = self.nc, self.k
        HALF = S // 2
        xbp = k.sb("xbp", [128, 2, S + 4], F32)
        xc = k.sb("xc", [128, 2, S], F32)
        abuf = k.sb("abuf", [128, HALF], F32)
        ubuf = k.sb("ubuf", [128, HALF], F32)
        cw = k.sb("cw", [128, 2, 4], F32)
        cb = k.sb("cb", [128, 2], F32)
        ba = k.sb("ba", [128, 2, 2], F32)
        bi = k.sb("bi", [128, 2, 2], F32)
        lam = k.sb("lam", [128, 2, 2], F32)
        sp1 = k.sb("sp1", [128, 4], F32)
        sp2 = k.sb("sp2", [128, 4], F32)
        tmp4 = [k.sb(f"tmp4_{i}", [128, 4], F32) for i in range(3)]
        wa = k.sb("wa", [128, 2, 2, 256], BF16)
        wi = k.sb("wi", [128, 2, 2, 256], BF16)
        xcb = [k.sb(f"xcb{i}", [128, 2, BLK], BF16) for i in range(2)]
        tr = [k.sb(f"tr{i}", [128, BLK], F32) for i in range(2)]
        ti = [k.sb(f"ti{i}", [128, BLK], F32) for i in range(2)]
        ta = [k.sb(f"ta{i}", [128, BLK], F32) for i in range(2)]
        tq = [k.sb(f"tq{i}", [128, BLK], F32) for i in range(2)]
        ybb = [k.sb(f"ybb{i}", [128, 2, BLK], F32) for i in range(2)]
        hl = k.sb("hl", [128, 1], F32)
        with nc.allow_non_contiguous_dma(reason="tiny parameter vectors"):
            self.dma("sp", cw[:], cw_d, writes=["cw"])
            self.dma("sp", cb[:], cb_d, writes=["cb"])
            self.dma("sp", ba[:], ba_d, writes=["ba"])
            self.dma("sp", bi[:], bi_d, writes=["bi"])
            self.dma("sp", lam[:], lam_d, writes=["lam"])
        for d in range(2):
            self.dma("pool", wa[:, d], wa_d[d].rearrange("(k p) n -> p k n", p=128), writes=[("wa", d)])
            self.dma("pool", wi[:, d], wi_d[d].rearrange("(k p) n -> p k n", p=128), writes=[("wi", d)])
        k.op("dve", lambda: nc.vector.memset(xbp[:, :, 0:2], 0.0), writes=["xpadl"])
        k.op("dve", lambda: nc.vector.memset(xbp[:, :, S + 2:S + 4], 0.0), writes=["xpadr"])
        for m in range(2):
            self.dma("sp", xbp[:, m, 2:S + 2], xb_d[m * 128:(m + 1) * 128, :], writes=[("xbp", m)])
        lamf = lam[:].rearrange("p d m -> p (d m)")
        k.op("act", lambda: nc.scalar.activation(out=tmp4[0][:], in_=lamf, func=AF.Abs),
             reads=["lam"], writes=["t40"])
        k.op("act", lambda: nc.scalar.activation(out=tmp4[1][:], in_=tmp4[0][:], func=AF.Exp, scale=-1.0),
             reads=["t40"], writes=["t41"])
        k.op("act", lambda: nc.scalar.activation(out=tmp4[1][:], in_=tmp4[1][:], func=AF.Ln, bias=1.0),
             reads=["t41"], writes=["t41"])
        k.op("dve", lambda: nc.vector.tensor_scalar(out=tmp4[2][:], in0=lamf, scalar1=-1.0, scalar2=0.0, op0=ALU.mult, op1=ALU.max),
             reads=["lam"], writes=["t42"])
        k.op("dve", lambda: nc.vector.tensor_tensor(out=tmp4[2][:], in0=tmp4[2][:], in1=tmp4[1][:], op=ALU.add),
             reads=["t42", "t41"], writes=["t42"])
        k.op("dve", lambda: nc.vector.tensor_scalar(out=sp1[:], in0=tmp4[2][:], scalar1=-LRU_C, scalar2=None, op0=ALU.mult),
             reads=["t42"], writes=["sp1"])
        k.op("dve", lambda: nc.vector.tensor_scalar(out=sp2[:], in0=tmp4[2][:], scalar1=-2.0 * LRU_C, scalar2=None, op0=ALU.mult),
             reads=["t42"], writes=["sp2"])
        for m in range(2):
            k.op("dve", lambda: nc.vector.tensor_scalar(out=xc[:, m, :], in0=xbp[:, m, 2:S + 2], scalar1=cw[:, m, 2:3], scalar2=cb[:, m:m + 1],
                                                        op0=ALU.mult, op1=ALU.add),
                 reads=[("xbp", m), "cw", "cb", "xpadl", "xpadr"], writes=[("xc", m)])
            for kk in (0, 1, 3):
                k.op("dve", lambda: nc.vector.scalar_tensor_tensor(out=xc[:, m, :], in0=xbp[:, m, kk:kk + S], scalar=cw[:, m, kk:kk + 1],
                                                                   in1=xc[:, m, :], op0=ALU.mult, op1=ALU.add),
                     reads=[("xbp", m), "cw", ("xc", m)], writes=[("xc", m)])
        acc = xbp
        NB = S // BLK
        for d in range(2):
            for m in range(2):
                col = d * 2 + m
                halves = (0, 1) if d == 0 else (1, 0)
                for hi, hf in enumerate(halves):
                    for tbi in range(NB // 2):
                        tb = hf * (NB // 2) + tbi
                        ts_ = slice(tb * BLK, (tb + 1) * BLK)
                        ls = slice(tbi * BLK, (tbi + 1) * BLK)
                        xi = self.ring("xcb", 2)
                        k.op("act", lambda: nc.scalar.copy(out=xcb[xi][:], in_=xc[:, :, ts_]),
                             reads=[("xc", 0), ("xc", 1)], writes=[("xcb", xi)])
                        j = self.ring("rg", 2)
                        pa, pi = self.psb[j], self.psb[2 + j]
                        for kc in range(2):
                            k.op("pe", lambda: nc.tensor.matmul(pa, lhsT=wa[:, d, kc, m * 128:(m + 1) * 128], rhs=xcb[xi][:, kc, :],
                                                                start=(kc == 0), stop=(kc == 1)),
                                 reads=[("wa", d), ("xcb", xi)], writes=[("ps", j)], inc=(kc == 1))
                        for kc in range(2):
                            k.op("pe", lambda: nc.tensor.matmul(pi, lhsT=wi[:, d, kc, m * 128:(m + 1) * 128], rhs=xcb[xi][:, kc, :],
                                                                start=(kc == 0), stop=(kc == 1)),
                                 reads=[("wi", d), ("xcb", xi)], writes=[("ps", 2 + j)], inc=(kc == 1))
                        t = self.ring("rt_", 2)
                        k.op("act", lambda: nc.scalar.activation(out=tr[t][:], in_=pa, func=AF.Sigmoid, bias=ba[:, d, m:m + 1]),
                             reads=[("ps", j), "ba"], writes=[("tr", t)])
                        k.op("act", lambda: nc.scalar.activation(out=ti[t][:], in_=pi, func=AF.Sigmoid, bias=bi[:, d, m:m + 1]),
                             reads=[("ps", 2 + j), "bi"], writes=[("ti", t)])
                        k.op("act", lambda: nc.scalar.activation(out=abuf[:, ls], in_=tr[t][:], func=AF.Exp, scale=sp1[:, col:col + 1]),
                             reads=[("tr", t), "sp1"], writes=[("abuf", tbi)])
                        k.op("act", lambda: nc.scalar.activation(out=ta[t][:], in_=tr[t][:], func=AF.Exp, scale=sp2[:, col:col + 1]),
                             reads=[("tr", t), "sp2"], writes=[("ta", t)])
                        k.op("dve", lambda: nc.vector.tensor_scalar(out=ta[t][:], in0=ta[t][:], scalar1=-1.0, scalar2=1.0, op0=ALU.mult, op1=ALU.add),
                             reads=[("ta", t)], writes=[("ta", t)])
                        k.op("act", lambda: nc.scalar.activation(out=tq[t][:], in_=ta[t][:], func=AF.Sqrt),
                             reads=[("ta", t)], writes=[("tq", t)])
                        k.op("dve", lambda: nc.vector.tensor_tensor(out=ti[t][:], in0=ti[t][:], in1=xc[:, m, ts_], op=ALU.mult),
                             reads=[("ti", t), ("xc", m)], writes=[("ti", t)])
                        k.op("dve", lambda: nc.vector.tensor_tensor(out=ubuf[:, ls], in0=ti[t][:], in1=tq[t][:], op=ALU.mult),
                             reads=[("ti", t), ("tq", t)], writes=[("ubuf", tbi)])
                    allb = [("abuf", i) for i in range(NB // 2)] + [("ubuf", i) for i in range(NB // 2)]
                    hs = slice(hf * HALF, (hf + 1) * HALF)
                    init = 0.0 if hi == 0 else hl[:, 0:1]
                    if d == 0:
                        k.op("dve", lambda: nc.vector.tensor_tensor_scan(out=acc[:, m, hs], data0=abuf[:], data1=ubuf[:], initial=init,
                                                                         op0=ALU.mult, op1=ALU.add),
                             reads=allb + ["hl", ("xbp", m)], writes=[("xbp", m)])
                        k.op("dve", lambda: nc.vector.tensor_copy(out=hl[:], in_=acc[:, m, (hf + 1) * HALF - 1:(hf + 1) * HALF]),
                             reads=[("xbp", m)], writes=["hl"])
                    else:
                        k.op("dve", lambda: nc.vector.tensor_tensor_scan(out=ubuf[:, ::-1], data0=abuf[:, ::-1], data1=ubuf[:, ::-1], initial=init,
                                                                         op0=ALU.mult, op1=ALU.add),
                             reads=allb + ["hl"], writes=[("ubuf", i) for i in range(NB // 2)])
                        k.op("dve", lambda: nc.vector.tensor_copy(out=hl[:], in_=ubuf[:, 0:1]),
                             reads=[("ubuf", 0)], writes=["hl"])
                        k.op("dve", lambda: nc.vector.tensor_tensor(out=acc[:, m, hs], in0=acc[:, m, hs], in1=ubuf[:], op=ALU.add),
                             reads=[("ubuf", i) for i in range(NB // 2)] + [("xbp", m)], writes=[("xbp", m)])
        yv = yb_d.rearrange("(m p) t -> p m t", p=128)
        ov = out_d.rearrange("(m p) t -> p m t", p=128)
        for tb in range(NB):
            ts_ = slice(tb * BLK, (tb + 1) * BLK)
            yi = self.ring("ybb", 2)
            self.dma("sp", ybb[yi][:], yv[:, :, ts_], writes=[("ybb", yi)])
            k.op("dve", lambda: nc.vector.tensor_tensor(out=ybb[yi][:], in0=ybb[yi][:], in1=acc[:, :, ts_], op=ALU.mult),
                 reads=[("ybb", yi), ("xbp", 0), ("xbp", 1)], writes=[("ybb", yi)])
            self.dma("sp", ov[:, :, ts_], ybb[yi][:], reads=[("ybb", yi)], writes=[("out_d", tb)])
        k.finish([("out_d", tb) for tb in range(NB)])


def build_ffn_only():
    nc = bass.Bass("TRN2", target_bir_lowering=False)
    x = nc.dram_tensor("xT", [D, TOK], F32, kind="ExternalInput").ap()
    gn = nc.dram_tensor("gain", [D], F32, kind="ExternalInput").ap()
    wg = nc.dram_tensor("wg", [D, DFF], F32, kind="ExternalInput").ap()
    wu = nc.dram_tensor("wu", [D, DFF], F32, kind="ExternalInput").ap()
    wd = nc.dram_tensor("wd", [DFF, D], F32, kind="ExternalInput").ap()
    y = nc.dram_tensor("yT", [D, TOK], F32, kind="ExternalOutput").ap()
    P = Prog(nc)
    P.alloc_x()
    P.alloc_norm()
    P.alloc_wring(2)
    P.alloc_ffn()
    P.load_x(x)
    g = P.load_vec("g_ffn", gn, KC)
    for b in range(NBLK):
        P.rmsnorm(b, g, "g_ffn")
        P.ffn(b, wg, wu, wd)
    P.store_x(y)
    P.k.close()
    return nc


def _consts(P, nc):
    ident = nc.dram_tensor("c_ident", [128, 128], F32, kind="ExternalInput").ap()
    P.dma("pool", P.ident[:], ident, writes=["ident"])


def build_attn_pre():
    nc = bass.Bass("TRN2", target_bir_lowering=False)
    x = nc.dram_tensor("xT", [D, TOK], F32, kind="ExternalInput").ap()
    gn = nc.dram_tensor("gain", [D], F32, kind="ExternalInput").ap()
    wqkv = nc.dram_tensor("wqkv", [D, QKV], F32, kind="ExternalInput").ap()
    qg_d = nc.dram_tensor("qg", [HD], F32, kind="ExternalInput").ap()
    kg_d = nc.dram_tensor("kg", [HD], F32, kind="ExternalInput").ap()
    cos_d = nc.dram_tensor("cosT", [128, TOK], F32, kind="ExternalInput").ap()
    sin_d = nc.dram_tensor("sinT", [128, TOK], F32, kind="ExternalInput").ap()
    rm_d = nc.dram_tensor("c_rm", [128, 128], F32, kind="ExternalInput").ap()
    q_out = nc.dram_tensor("q_out", [NH, 128, TOK], BF16, kind="ExternalOutput").ap()
    k_out = nc.dram_tensor("k_out", [NKV, 128, TOK], BF16, kind="ExternalOutput").ap()
    v_out = nc.dram_tensor("v_out", [TOK, NKV * HD], BF16, kind="ExternalOutput").ap()
    P = Prog(nc)
    P.alloc_x()
    P.alloc_norm()
    P.alloc_attn_pre()
    P.load_x(x)
    g = P.load_vec("g_mix", gn, KC)
    qg = P.load_vec("qg", qg_d, 1)
    kg = P.load_vec("kg", kg_d, 1)
    P.dma("sp", P.cosT[:], cos_d, writes=["cosT"])
    P.dma("sp", P.sinT[:], sin_d, writes=["sinT"])
    P.dma("sp", P.rm[:], rm_d, writes=["rm"])
    P.attn_pre(wqkv, g, "g_mix", qg, kg, k_out, v_out)
    for h in range(NH):
        P.dma("sp", q_out[h], P.qT[:, h, :], reads=[("q", h)], writes=[("q_out", h)])
    P.k.finish([("q_out", h) for h in range(NH)] + [("kT_out", g_) for g_ in range(NKV)] + ["v_out"])
    P.k.close()
    return nc


def build_attn_core(debug=False, **kw):
    nc = bass.Bass("TRN2", target_bir_lowering=False)
    x = nc.dram_tensor("xT", [D, TOK], F32, kind="ExternalInput").ap()
    q_in = nc.dram_tensor("q_in", [NH, 128, TOK], BF16, kind="ExternalInput").ap()
    k_all = nc.dram_tensor("k_all", [NCORES, NKV, 128, TOK], BF16, kind="ExternalInput").ap()
    v_all = nc.dram_tensor("v_all", [NCORES, TOK, NKV * HD], BF16, kind="ExternalInput").ap()
    wo = nc.dram_tensor("wo", [D, D], F32, kind="ExternalInput").ap()
    y = nc.dram_tensor("yT", [D, TOK], F32, kind="ExternalOutput").ap()
    P = Prog(nc)
    P.alloc_x()
    P.qT = P.k.sb("qT", [128, NH, TOK], BF16)
    P.alloc_attn_core()
    P.alloc_proj()
    P.load_x(x)
    for h in range(NH):
        P.dma("sp", P.qT[:, h, :], q_in[h], writes=[("q", h)])
    P.attn_core(k_all, v_all, **kw)
    if debug:
        o_dbg = nc.dram_tensor("o_dbg", [NH, 128, TOK], BF16, kind="ExternalOutput").ap()
        for h in range(NH):
            P.dma("sp", o_dbg[h], P.qT[:, h, :], reads=[("q", h)], writes=[("o_dbg", h)])
        P.k.finish([("o_dbg", h) for h in range(NH)])
    P.proj_resid(P.qT, lambda kc: ("q", kc), wo)
    P.store_x(y)
    P.k.close()
    return nc


def build_rnn_pre():
    nc = bass.Bass("TRN2", target_bir_lowering=False)
    x = nc.dram_tensor("xT", [D, TOK], F32, kind="ExternalInput").ap()
    gn = nc.dram_tensor("gain", [D], F32, kind="ExternalInput").ap()
    w_in = nc.dram_tensor("w_in", [D, 2 * D], F32, kind="ExternalInput").ap()
    xy = nc.dram_tensor("xy_out", [2 * D, TOK], F32, kind="ExternalOutput").ap()
    P = Prog(nc)
    P.alloc_x()
    P.alloc_norm()
    P.wq = [P.k.sb(f"wq{i}", [128, KC, 512], BF16) for i in range(2)]
    P.load_x(x)
    g = P.load_vec("g_mix", gn, KC)
    P.rnn_pre(w_in, g, "g_mix", xy)
    P.k.close()
    return nc


def build_rnn_core():
    nc = bass.Bass("TRN2", target_bir_lowering=False)
    xb = nc.dram_tensor("xb", [256, S], F32, kind="ExternalInput").ap()
    yb = nc.dram_tensor("yb", [256, S], F32, kind="ExternalInput").ap()
    cw = nc.dram_tensor("cw", [128, 2, 4], F32, kind="ExternalInput").ap()
    cb = nc.dram_tensor("cb", [128, 2], F32, kind="ExternalInput").ap()
    wa = nc.dram_tensor("wa", [2, 256, 256], F32, kind="ExternalInput").ap()
    ba = nc.dram_tensor("ba", [128, 2, 2], F32, kind="ExternalInput").ap()
    wi = nc.dram_tensor("wi", [2, 256, 256], F32, kind="ExternalInput").ap()
    bi = nc.dram_tensor("bi", [128, 2, 2], F32, kind="ExternalInput").ap()
    lam = nc.dram_tensor("lam", [128, 2, 2], F32, kind="ExternalInput").ap()
    out = nc.dram_tensor("rout", [256, S], F32, kind="ExternalOutput").ap()
    P = Prog(nc)
    P.rnn_core(xb, yb, cw, cb, wa, ba, wi, bi, lam, out)
    P.k.close()
    return nc


def build_rnn_post():
    nc = bass.Bass("TRN2", target_bir_lowering=False)
    x = nc.dram_tensor("xT", [D, TOK], F32, kind="ExternalInput").ap()
    r = nc.dram_tensor("rT", [D, TOK], F32, kind="ExternalInput").ap()
    wo = nc.dram_tensor("wo", [D, D], F32, kind="ExternalInput").ap()
    y = nc.dram_tensor("yT", [D, TOK], F32, kind="ExternalOutput").ap()
    P = Prog(nc)
    P.alloc_x()
    P.qT = P.k.sb("qT", [128, KC, TOK], BF16)
    P.alloc_proj()
    P.load_x(x)
    rv = r.rearrange("(k p) t -> p k t", p=128)
    for kc in range(KC):
        P.dma("pool", P.qT[:, kc, :], rv[:, kc, :], writes=[("q", kc)])
    P.proj_resid(P.qT, lambda kc: ("q", kc), wo)
    P.store_x(y)
    P.k.close()
    return nc


def _pm(v):
    v = np.asarray(v, np.float32)
    lead = v.shape[:-1]
    return np.ascontiguousarray(np.moveaxis(v.reshape(*lead, 2, 128), -1, 0))


def rnn_small_params(cw, cb, ba, bi, lam):
    return {"cw": np.ascontiguousarray(_pm(cw).transpose(0, 2, 1)), "cb": _pm(cb), "ba": _pm(ba), "bi": _pm(bi), "lam": _pm(lam)}


def _rope_tables():
    f = (10000.0 ** (-np.arange(32, dtype=np.float32) / 32)).astype(np.float32)
    t = np.arange(S)
    row = (t // GRID_W).astype(np.float32)[:, None] * f
    col = (t % GRID_W).astype(np.float32)[:, None] * f
    cos = np.concatenate([np.cos(row), np.cos(row), np.cos(col), np.cos(col)], 1).astype(np.float32)
    sin = np.concatenate([-np.sin(row), np.sin(row), -np.sin(col), np.sin(col)], 1).astype(np.float32)
    rm = np.zeros((128, 128), np.float32)
    for dd in range(128):
        p = dd + 32 if (dd % 64) < 32 else dd - 32
        rm[p, dd] = 1.0
    return cos, sin, rm


def _run(nc, ins):
    res = run_bass_kernel_spmd(nc, ins, core_ids=list(range(NCORES)))
    return res.results


def kernel(x, norm_mix, norm_ffn, attn_w_qkv, attn_q_gain, attn_k_gain, attn_w_o,
           rnn_w_in, rnn_conv_w, rnn_conv_b, rnn_w_a, rnn_b_a, rnn_w_i, rnn_b_i,
           rnn_lambda, rnn_w_out, ffn_w_gate, ffn_w_up, ffn_w_down):
    f32 = lambda a: np.ascontiguousarray(np.asarray(a, dtype=np.float32))
    x = f32(x)
    cos, sin, rm = _rope_tables()
    xs = [np.ascontiguousarray(x[0, c * TOK:(c + 1) * TOK, :].T) for c in range(NCORES)]
    cosT = [np.ascontiguousarray(cos[c * TOK:(c + 1) * TOK].T) for c in range(NCORES)]
    sinT = [np.ascontiguousarray(sin[c * TOK:(c + 1) * TOK].T) for c in range(NCORES)]
    for i in range(4):
        j = i // 2
        if i % 2 == 0:
            wqkv, wo = f32(attn_w_qkv[j]), f32(attn_w_o[j])
            r = _run(build_attn_pre(), [{"xT": xs[c], "gain": f32(norm_mix[i]), "wqkv": wqkv, "qg": f32(attn_q_gain[j]),
                                         "kg": f32(attn_k_gain[j]), "cosT": cosT[c], "sinT": sinT[c], "c_rm": rm}
                                        for c in range(NCORES)])
            k_all = np.stack([r[c]["k_out"] for c in range(NCORES)])
            v_all = np.stack([r[c]["v_out"] for c in range(NCORES)])
            r2 = _run(build_attn_core(), [{"xT": xs[c], "q_in": r[c]["q_out"], "k_all": k_all, "v_all": v_all, "wo": wo}
                                          for c in range(NCORES)])
            xs = [r2[c]["yT"] for c in range(NCORES)]
        else:
            r = _run(build_rnn_pre(), [{"xT": xs[c], "gain": f32(norm_mix[i]), "w_in": f32(rnn_w_in[j])} for c in range(NCORES)])
            xy = np.concatenate([r[c]["xy_out"] for c in range(NCORES)], axis=1)
            ins = []
            for c in range(NCORES):
                sl = slice(c * 256, (c + 1) * 256)
                ins.append({"xb": np.ascontiguousarray(xy[c * 256:(c + 1) * 256]),
                            "yb": np.ascontiguousarray(xy[D + c * 256:D + (c + 1) * 256]),
                            "wa": f32(rnn_w_a[j][:, c]), "wi": f32(rnn_w_i[j][:, c]),
                            **rnn_small_params(f32(rnn_conv_w[j])[:, sl], f32(rnn_conv_b[j])[sl], f32(rnn_b_a[j])[:, sl],
                                               f32(rnn_b_i[j])[:, sl], f32(rnn_lambda[j])[:, sl])})
            r2 = _run(build_rnn_core(), ins)
            rr = np.concatenate([r2[c]["rout"] for c in range(NCORES)], axis=0)
            r3 = _run(build_rnn_post(), [{"xT": xs[c], "rT": np.ascontiguousarray(rr[:, c * TOK:(c + 1) * TOK]),
                                          "wo": f32(rnn_w_out[j])} for c in range(NCORES)])
            xs = [r3[c]["yT"] for c in range(NCORES)]
        r4 = _run(build_ffn_only(), [{"xT": xs[c], "gain": f32(norm_ffn[i]), "wg": f32(ffn_w_gate[i]), "wu": f32(ffn_w_up[i]),
                                      "wd": f32(ffn_w_down[i])} for c in range(NCORES)])
        xs = [r4[c]["yT"] for c in range(NCORES)]
    out = np.empty((1, S, D), np.float32)
    for c in range(NCORES):
        out[0, c * TOK:(c + 1) * TOK, :] = np.asarray(xs[c], np.float32).T
    return out
```

```python
import contextlib
import numpy as np
import concourse.bass as bass
import concourse.mybir as mybir
from concourse.bass_utils import run_bass_kernel_spmd

F32 = mybir.dt.float32
BF16 = mybir.dt.bfloat16
AF = mybir.ActivationFunctionType
ALU = mybir.AluOpType

NCORES = 8
D = 2048
S = 8192
TOK = S // NCORES
KC = D // 128
DFF = 5632
FC = DFF // 128
NH = 16
NKV = 4
HD = 128
QKV = (NH + 2 * NKV) * HD
EPS = 1e-6
GRID_W = 64
BLK = 512
NBLK = TOK // BLK
LRU_C = 8.0


class Tok:
    __slots__ = ("eng", "sem", "val")

    def __init__(self, eng, sem, val):
        self.eng, self.sem, self.val = eng, sem, val


class EngS:
    def __init__(self, name, eng, sem):
        self.name, self.eng, self.sem = name, eng, sem
        self.count = 0
        self.pending = []
        self.waited = {}
        self.dma_sems = []
        self.dma_vals = []
        self.dma_i = 0


class Res:
    __slots__ = ("w", "r")

    def __init__(self):
        self.w = None
        self.r = {}


class K:
    def __init__(self, nc, n_dma_sems=20):
        self.nc = nc
        self.es = contextlib.ExitStack()
        self.res = {}
        self.engs = {}
        for name, eng in (("pe", nc.tensor), ("act", nc.scalar), ("dve", nc.vector),
                          ("pool", nc.gpsimd), ("sp", nc.sync)):
            sem = self.es.enter_context(nc.semaphore("s_" + name))
            self.engs[name] = EngS(name, eng, sem)
        for name in ("pool", "sp"):
            E = self.engs[name]
            for i in range(n_dma_sems):
                E.dma_sems.append(self.es.enter_context(nc.semaphore(f"d_{name}{i}")))
                E.dma_vals.append(0)
        self.cc_sem = self.es.enter_context(nc.semaphore("s_cc"))
        self.cc_count = 0
        self.n_wait = 0
        self.log = []
        self.n_ins = 0

    def sb(self, name, shape, dt):
        return self.es.enter_context(self.nc.sbuf_tensor("sb_" + name, list(shape), dt))

    def ps(self, name, shape, dt):
        return self.es.enter_context(self.nc.psum_tensor(name, list(shape), dt))

    def _wait(self, E, tok):
        if tok.val is None:
            raise RuntimeError("dependency on an instruction without milestone (engine %s)" % tok.eng)
        key = id(tok.sem)
        if E.waited.get(key, 0) >= tok.val:
            return
        E.eng.wait_ge(tok.sem, tok.val)
        self.log.append((E.name, 'wait', tok.eng, tok.val))
        E.waited[key] = tok.val
        self.n_wait += 1

    def op(self, engname, emit, reads=(), writes=(), inc=True, dma=False, cc=False):
        E = self.engs[engname]
        deps = []
        for r in reads:
            st = self.res.get(r)
            if st is not None and st.w is not None:
                deps.append(st.w)
        for w in writes:
            st = self.res.get(w)
            if st is not None:
                if st.w is not None:
                    deps.append(st.w)
                deps.extend(st.r.values())
        for tok in deps:
            if engname == "pe" and tok.eng == "pe":
                continue
            self._wait(E, tok)
        if cc:
            ins = emit()
            self.cc_count += 1
            ins.then_inc(self.cc_sem)
            tok = Tok("cc", self.cc_sem, self.cc_count)
        elif dma:
            i = E.dma_i % len(E.dma_sems)
            E.dma_i += 1
            sem = E.dma_sems[i]
            if E.dma_vals[i] > 0:
                self._wait(E, Tok("dma", sem, E.dma_vals[i]))
            ins = emit()
            E.dma_vals[i] += 16
            ins.then_inc(sem, 16)
            tok = Tok("dma%d" % id(sem), sem, E.dma_vals[i])
        else:
            ins = emit()
            if inc:
                E.count += 1
                ins.then_inc(E.sem, 1)
                tok = Tok(engname, E.sem, E.count)
                for p in E.pending:
                    p.val = E.count
                E.pending = []
            else:
                tok = Tok(engname, E.sem, None)
                E.pending.append(tok)
        self.n_ins += 1
        self.log.append((engname, 'op', tuple(writes), tok.val))
        for w in writes:
            st = self.res.setdefault(w, Res())
            st.w = tok
            st.r = {}
        for r in reads:
            st = self.res.setdefault(r, Res())
            st.r[tok.eng] = tok
        return tok

    def barrier(self):
        engs = list(self.engs.values())
        for E in engs:
            if E.pending:
                raise RuntimeError("barrier with pending non-milestone instructions on " + E.name)
        for E in engs:
            for E2 in engs:
                if E2 is E or E2.count == 0:
                    continue
                self._wait(E, Tok(E2.name, E2.sem, E2.count))
            for E2 in engs:
                for sem, val in zip(E2.dma_sems, E2.dma_vals):
                    if val > 0:
                        self._wait(E, Tok("dma", sem, val))
            if self.cc_count:
                self._wait(E, Tok("cc", self.cc_sem, self.cc_count))
        self.res = {}

    def finish(self, final_resources):
        E = self.engs["sp"]
        for r in final_resources:
            st = self.res.get(r)
            if st is not None and st.w is not None:
                self._wait(E, st.w)

    def close(self):
        self.es.close()


class Prog:
    def __init__(self, nc):
        self.nc = nc
        self.k = K(nc)
        k = self.k
        self.ones_bf = k.sb("ones_bf", [128, 128], BF16)
        self.ones_f = k.sb("ones_f", [128, 128], F32)
        self.eps_t = k.sb("eps_t", [128, 1], F32)
        k.op("dve", lambda: nc.vector.memset(self.ones_bf[:], 1.0), writes=["ones_bf"])
        k.op("dve", lambda: nc.vector.memset(self.ones_f[:], 1.0), writes=["ones_f"])
        k.op("dve", lambda: nc.vector.memset(self.eps_t[:], EPS), writes=["eps_t"])
        self.psall = k.ps("psall", [128, 8, 512], F32)
        self.psb = [self.psall[:, i, :] for i in range(8)]
        self.rr = {}
        self.arena = None

    def make_arena(self, nbytes):
        self.arena_bytes = nbytes
        self.arena = self.k.sb("arena", [128, nbytes // 4], F32)

    def phase(self, specs, barrier=True):
        if barrier:
            self.k.barrier()
        off = 0
        for sp in specs:
            name, shape, dt_ = sp[0], sp[1], sp[2]
            cnt = sp[3] if len(sp) > 3 else None
            views = []
            for _ in range(cnt or 1):
                n = int(np.prod(shape[1:]))
                nb = n * (2 if dt_ == BF16 else 4)
                nb_al = (nb + 63) // 64 * 64
                a = self.arena[:, off // 4:(off + nb_al) // 4]
                if dt_ == BF16:
                    a = a.bitcast(BF16)
                a = a[:, 0:n]
                if len(shape) == 3:
                    a = a.rearrange("p (a b) -> p a b", a=shape[1])
                elif len(shape) == 4:
                    a = a.rearrange("p (a b c) -> p a b c", a=shape[1], b=shape[2])
                elif len(shape) == 5:
                    a = a.rearrange("p (a b c d) -> p a b c d", a=shape[1], b=shape[2], c=shape[3])
                views.append(a)
                off += nb_al
            setattr(self, name, views if cnt else views[0])
        if off > self.arena_bytes:
            raise RuntimeError(f"arena overflow: {off} > {self.arena_bytes}")
        return off

    def ring(self, name, n):
        i = self.rr.get(name, 0)
        self.rr[name] = i + 1
        return i % n

    def dma(self, q, out, in_, reads=(), writes=()):
        eng = self.nc.gpsimd if q == "pool" else self.nc.sync
        return self.k.op(q, lambda: eng.dma_start(out=out, in_=in_), reads=reads, writes=writes, dma=True)

    def alloc_x(self):
        self.xT = self.k.sb("xT", [128, KC, TOK], F32)

    def load_x(self, x_dram):
        src = x_dram.rearrange("(k p) t -> p k t", p=128)
        for kc in range(KC):
            self.dma("sp", self.xT[:, kc, :], src[:, kc, :],
                     writes=[("x", kc, b) for b in range(NBLK)])

    def store_x(self, y_dram):
        dst = y_dram.rearrange("(k p) t -> p k t", p=128)
        for kc in range(KC):
            self.dma("sp", dst[:, kc, :], self.xT[:, kc, :],
                     reads=[("x", kc, b) for b in range(NBLK)], writes=[("yout", kc)])
        self.k.finish([("yout", kc) for kc in range(KC)])

    def load_vec(self, name, dram_1d, n):
        t = self.k.sb(name, [128, n], F32)
        with self.nc.allow_non_contiguous_dma(reason="tiny per-feature vector"):
            self.dma("sp", t[:], dram_1d.rearrange("(c p) -> p c", p=128), writes=[name])
        return t

    SP_NORM = [("hT", [128, KC, BLK], BF16), ("sq", [128, BLK], BF16, 3), ("rt", [128, BLK], F32), ("rstd", [128, BLK], F32)]

    def rmsnorm(self, b, gain_t, gain_name, ps_i=7):
        nc, k = self.nc, self.k
        ps = self.psb[ps_i]
        bs = slice(b * BLK, (b + 1) * BLK)
        for kc in range(KC):
            i = self.ring("sq", 3)
            k.op("act", lambda: nc.scalar.activation(out=self.sq[i][:], in_=self.xT[:, kc, bs], func=AF.Square),
                 reads=[("x", kc, b)], writes=[("sq", i)])
            k.op("pe", lambda: nc.tensor.matmul(ps, lhsT=self.ones_bf[:], rhs=self.sq[i][:],
                                                start=(kc == 0), stop=(kc == KC - 1)),
                 reads=[("sq", i), "ones_bf"], writes=[("ps", ps_i)])
        k.op("act", lambda: nc.scalar.activation(out=self.rt[:], in_=ps, func=AF.Sqrt,
                                                 bias=self.eps_t[:], scale=1.0 / D),
             reads=[("ps", ps_i), "eps_t"], writes=["rt"])
        k.op("dve", lambda: nc.vector.reciprocal(out=self.rstd[:], in_=self.rt[:]),
             reads=["rt"], writes=["rstd"])
        for kc in range(KC):
            k.op("dve", lambda: nc.vector.scalar_tensor_tensor(
                out=self.hT[:, kc, :], in0=self.xT[:, kc, bs], scalar=gain_t[:, kc:kc + 1],
                in1=self.rstd[:], op0=ALU.mult, op1=ALU.mult),
                 reads=[("x", kc, b), "rstd", gain_name], writes=[("h", kc)])

    WSLOT = FC * 256

    SP_WRING = [("wbuf", [128, FC * 256], BF16, 2)]
    nw = 2

    def wslot(self):
        return self.ring("w", self.nw)

    SP_FFN = [("aT", [128, FC, BLK], BF16), ("sg", [128, BLK], F32, 2)]

    def ffn(self, b, wg, wu, wd):
        nc, k = self.nc, self.k
        bs = slice(b * BLK, (b + 1) * BLK)
        wg_v = wg.rearrange("(k p) n -> p k n", p=128)
        wu_v = wu.rearrange("(k p) n -> p k n", p=128)
        wd_v = wd.rearrange("(c p) n -> p c n", p=128)
        CW = 256
        for pr in range(DFF // CW):
            s = self.wslot()
            wgb = self.wbuf[s][:, 0:KC * CW].rearrange("p (k n) -> p k n", k=KC)
            wub = self.wbuf[s][:, KC * CW:2 * KC * CW].rearrange("p (k n) -> p k n", k=KC)
            self.dma("pool", wgb, wg_v[:, :, pr * CW:(pr + 1) * CW], writes=[("w", s, 0)])
            self.dma("pool", wub, wu_v[:, :, pr * CW:(pr + 1) * CW], writes=[("w", s, 1)])
            for c in range(CW // 128):
                cc = pr * (CW // 128) + c
                j = self.ring("gu", 2)
                pg, pu = self.psb[j], self.psb[2 + j]
                for kc in range(KC):
                    k.op("pe", lambda: nc.tensor.matmul(pg, lhsT=wgb[:, kc, c * 128:(c + 1) * 128],
                                                        rhs=self.hT[:, kc, :], start=(kc == 0), stop=(kc == KC - 1)),
                         reads=[("w", s, 0), ("h", kc)], writes=[("ps", j)], inc=(kc == KC - 1))
                for kc in range(KC):
                    k.op("pe", lambda: nc.tensor.matmul(pu, lhsT=wub[:, kc, c * 128:(c + 1) * 128],
                                                        rhs=self.hT[:, kc, :], start=(kc == 0), stop=(kc == KC - 1)),
                         reads=[("w", s, 1), ("h", kc)], writes=[("ps", 2 + j)], inc=(kc == KC - 1))
                g = self.ring("sg", 2)
                k.op("act", lambda: nc.scalar.activation(out=self.sg[g][:], in_=pg, func=AF.Silu),
                     reads=[("ps", j)], writes=[("sg", g)])
                k.op("dve", lambda: nc.vector.tensor_tensor(out=self.aT[:, cc, :], in0=pu, in1=self.sg[g][:],
                                                            op=ALU.mult),
                     reads=[("ps", 2 + j), ("sg", g)], writes=[("a", cc)])
        for pr in range(D // CW):
            s = self.wslot()
            wdb = self.wbuf[s][:, 0:FC * CW].rearrange("p (c n) -> p c n", c=FC)
            self.dma("pool", wdb, wd_v[:, :, pr * CW:(pr + 1) * CW], writes=[("w", s, 0), ("w", s, 1)])
            for c in range(CW // 128):
                jj = pr * (CW // 128) + c
                j = 4 + self.ring("dn", 2)
                pd = self.psb[j]
                for fc in range(FC):
                    k.op("pe", lambda: nc.tensor.matmul(pd, lhsT=wdb[:, fc, c * 128:(c + 1) * 128],
                                                        rhs=self.aT[:, fc, :], start=(fc == 0), stop=(fc == FC - 1)),
                         reads=[("w", s, 0), ("a", fc)], writes=[("ps", j)], inc=(fc == FC - 1))
                k.op("dve", lambda: nc.vector.tensor_tensor(out=self.xT[:, jj, bs], in0=pd, in1=self.xT[:, jj, bs],
                                                            op=ALU.add),
                     reads=[("ps", j), ("x", jj, b)], writes=[("x", jj, b)])


    SP_PROJ = [("pw", [128, KC, 256], BF16, 2)]

    def proj_resid(self, inT, in_res, w):
        nc, k = self.nc, self.k
        w_v = w.rearrange("(k p) n -> p k n", p=128)
        CW = 256
        for pr in range(D // CW):
            s = self.ring("pw", 2)
            self.dma("pool", self.pw[s][:], w_v[:, :, pr * CW:(pr + 1) * CW], writes=[("pw", s)])
            for c in range(CW // 128):
                jj = pr * (CW // 128) + c
                for b in range(NBLK):
                    bs = slice(b * BLK, (b + 1) * BLK)
                    j = 4 + self.ring("dn", 2)
                    pd = self.psb[j]
                    for kc in range(KC):
                        k.op("pe", lambda: nc.tensor.matmul(pd, lhsT=self.pw[s][:, kc, c * 128:(c + 1) * 128],
                                                            rhs=inT[:, kc, bs], start=(kc == 0), stop=(kc == KC - 1)),
                             reads=[("pw", s)] + list(in_res(kc)), writes=[("ps", j)], inc=(kc == KC - 1))
                    k.op("dve", lambda: nc.vector.tensor_tensor(out=self.xT[:, jj, bs], in0=pd, in1=self.xT[:, jj, bs],
                                                                op=ALU.add),
                         reads=[("ps", j), ("x", jj, b)], writes=[("x", jj, b)])

    SP_QT = [("qT", [128, NH, TOK], BF16)]
    SP_ATTN_PRE = [("kTl", [128, NKV, TOK], BF16), ("vl", [128, TOK // 128, NKV * HD], BF16),
                   ("cosT", [128, TOK], F32), ("sinT", [128, TOK], F32), ("rm", [128, 128], F32),
                   ("sqf", [128, BLK], F32, 2), ("rt2", [128, BLK], F32), ("rs2", [128, BLK], F32),
                   ("qn", [128, BLK], F32, 2), ("t1", [128, BLK], F32), ("t2", [128, BLK], F32),
                   ("wq", [128, KC, 512], BF16, 2)]

    def attn_pre(self, wqkv, g_mix, g_mix_name, qg, kg, kT_out, v_out, cos_d, sin_d, rm_d):
        nc, k = self.nc, self.k
        self.dma("sp", self.cosT[:], cos_d, writes=["cosT"])
        self.dma("sp", self.sinT[:], sin_d, writes=["sinT"])
        self.dma("sp", self.rm[:], rm_d, writes=["rm"])
        w_v = wqkv.rearrange("(k p) n -> p k n", p=128)
        for b in range(NBLK):
            bs = slice(b * BLK, (b + 1) * BLK)
            self.rmsnorm(b, g_mix, g_mix_name)
            stA, stB, stC = {}, {}, {}

            def A(c, ws):
                j = c % 2
                pq = self.psb[j]
                for kc in range(KC):
                    k.op("pe", lambda: nc.tensor.matmul(pq, lhsT=self.wq[ws][:, kc, (c % 4) * 128:(c % 4 + 1) * 128],
                                                        rhs=self.hT[:, kc, :], start=(kc == 0), stop=(kc == KC - 1)),
                         reads=[("wq", ws), ("h", kc)], writes=[("ps", j)], inc=(kc == KC - 1))
                k.op("act", lambda: nc.scalar.activation(out=self.sqf[j][:], in_=pq, func=AF.Square),
                     reads=[("ps", j)], writes=[("sqf", j)])

            def B(c):
                j = c % 2
                pq = self.psb[j]
                gt, gname = (qg, "qg") if c < NH else (kg, "kg")
                k.op("pe", lambda: nc.tensor.matmul(self.psb[2], lhsT=self.ones_f[:], rhs=self.sqf[j][:], start=True, stop=True),
                     reads=[("sqf", j), "ones_f"], writes=[("ps", 2)])
                k.op("act", lambda: nc.scalar.activation(out=self.rt2[:], in_=self.psb[2], func=AF.Sqrt,
                                                         bias=self.eps_t[:], scale=1.0 / HD),
                     reads=[("ps", 2), "eps_t"], writes=["rt2"])
                k.op("dve", lambda: nc.vector.reciprocal(out=self.rs2[:], in_=self.rt2[:]), reads=["rt2"], writes=["rs2"])
                k.op("dve", lambda: nc.vector.scalar_tensor_tensor(out=self.qn[j][:], in0=pq, scalar=gt[:, 0:1], in1=self.rs2[:],
                                                                   op0=ALU.mult, op1=ALU.mult),
                     reads=[("ps", j), "rs2", gname], writes=[("qn", j)])

            def C(c):
                j = c % 2
                k.op("pe", lambda: nc.tensor.matmul(self.psb[3], lhsT=self.rm[:], rhs=self.qn[j][:], start=True, stop=True),
                     reads=[("qn", j), "rm"], writes=[("ps", 3)])
                k.op("dve", lambda: nc.vector.tensor_tensor(out=self.t1[:], in0=self.qn[j][:], in1=self.cosT[:, bs], op=ALU.mult),
                     reads=[("qn", j), "cosT"], writes=["t1"])
                k.op("dve", lambda: nc.vector.tensor_tensor(out=self.t2[:], in0=self.psb[3], in1=self.sinT[:, bs], op=ALU.mult),
                     reads=[("ps", 3), "sinT"], writes=["t2"])
                if c < NH:
                    dst, wr = self.qT[:, c, bs], [("q", c)]
                else:
                    dst, wr = self.kTl[:, c - NH, bs], [("kTl", c - NH)]
                k.op("dve", lambda: nc.vector.tensor_tensor(out=dst, in0=self.t1[:], in1=self.t2[:], op=ALU.add),
                     reads=["t1", "t2"], writes=wr)

            ws = 0
            nqk = NH + NKV
            for c in range(nqk + 2):
                if c < nqk:
                    if c % 4 == 0:
                        ws = self.ring("wq", 2)
                        self.dma("pool", self.wq[ws][:], w_v[:, :, c * 128:c * 128 + 512], writes=[("wq", ws)])
                    A(c, ws)
                if 0 <= c - 1 < nqk:
                    B(c - 1)
                if 0 <= c - 2 < nqk:
                    C(c - 2)
            ws = self.ring("wq", 2)
            self.dma("pool", self.wq[ws][:], w_v[:, :, nqk * 128:nqk * 128 + 512], writes=[("wq", ws)])
            for tt in range(BLK // 128):
                tg = b * (BLK // 128) + tt
                j = 4 + self.ring("dn", 2)
                pv = self.psb[j]
                for kc in range(KC):
                    k.op("pe", lambda: nc.tensor.matmul(pv, lhsT=self.hT[:, kc, tt * 128:(tt + 1) * 128],
                                                        rhs=self.wq[ws][:, kc, :], start=(kc == 0), stop=(kc == KC - 1)),
                         reads=[("wq", ws), ("h", kc)], writes=[("ps", j)], inc=(kc == KC - 1))
                k.op("act", lambda: nc.scalar.copy(out=self.vl[:, tg, :], in_=pv), reads=[("ps", j)], writes=[("vl", tg)])
        for g in range(NKV):
            self.dma("sp", kT_out[g], self.kTl[:, g, :], reads=[("kTl", g)], writes=[("kT_out", g)])
        self.dma("sp", v_out.rearrange("(t p) n -> p t n", p=128), self.vl[:],
                 reads=[("vl", t) for t in range(TOK // 128)], writes=["v_out"])

    SP_ATTN_CORE = [("ktb", [128, S], BF16, 2), ("vb", [128, S // 128, HD], BF16, 2), ("pT", [128, TOK], BF16, 3),
                    ("rz", [128, TOK], F32)]

    def attn_core(self, kT_all, v_all, dbg_heads=NH, dbg_nkt=S // 128):
        nc, k = self.nc, self.k
        NKT = dbg_nkt
        scale = float(HD) ** -0.5
        ps_s = [self.psall[:, 0:2, :], self.psall[:, 2:4, :]]
        ps_o = self.psall[:, 4:6, :]
        ps_z = self.psall[:, 6:8, :]
        steps = [(hh, kt) for hh in range(dbg_heads) for kt in range(NKT)]

        def load_kv(g):
            gb = g % 2
            self.dma("sp", self.ktb[gb][:].rearrange("p (r t) -> p r t", r=NCORES),
                     kT_all[:, g].rearrange("r p t -> p r t"), reads=["k_all"], writes=[("kt", gb)])
            for r in range(NCORES):
                self.dma("sp", self.vb[gb][:, r * 8:(r + 1) * 8, :],
                         v_all[r, :, g * HD:(g + 1) * HD].rearrange("(t p) d -> p t d", p=128),
                         reads=["v_all"], writes=[("v", gb, r)])

        def QK(i):
            hh, kt = steps[i]
            g = hh // 4
            sb_ = i % 2
            for qh in range(2):
                k.op("pe", lambda: nc.tensor.matmul(ps_s[sb_][:, qh, :], lhsT=self.ktb[g % 2][:, kt * 128:(kt + 1) * 128],
                                                    rhs=self.qT[:, hh, qh * 512:(qh + 1) * 512], start=True, stop=True),
                     reads=[("kt", g % 2), ("q", hh)], writes=[("ps", 2 * sb_), ("ps", 2 * sb_ + 1)], inc=(qh == 1))

        def EXP(i):
            sb_ = i % 2
            pb = i % 3
            k.op("act", lambda: nc.scalar.activation(out=self.pT[pb][:], in_=ps_s[sb_].rearrange("p a b -> p (a b)"),
                                                     func=AF.Exp, scale=scale),
                 reads=[("ps", 2 * sb_), ("ps", 2 * sb_ + 1)], writes=[("pT", pb)])

        def PV(i):
            hh, kt = steps[i]
            g = hh // 4
            pb = i % 3
            for qh in range(2):
                k.op("pe", lambda: nc.tensor.matmul(ps_o[:, qh, :], lhsT=self.vb[g % 2][:, kt, :],
                                                    rhs=self.pT[pb][:, qh * 512:(qh + 1) * 512], start=(kt == 0), stop=(kt == NKT - 1)),
                     reads=[("pT", pb), ("v", g % 2, kt // 8)], writes=[("ps", 4), ("ps", 5)], inc=False)
                k.op("pe", lambda: nc.tensor.matmul(ps_z[:, qh, :], lhsT=self.ones_bf[:],
                                                    rhs=self.pT[pb][:, qh * 512:(qh + 1) * 512], start=(kt == 0), stop=(kt == NKT - 1)),
                     reads=[("pT", pb), "ones_bf"], writes=[("ps", 6), ("ps", 7)], inc=(qh == 1))

        def FIN(hh):
            k.op("dve", lambda: nc.vector.reciprocal(out=self.rz[:], in_=ps_z.rearrange("p a b -> p (a b)")),
                 reads=[("ps", 6), ("ps", 7)], writes=["rz"])
            k.op("dve", lambda: nc.vector.tensor_tensor(out=self.qT[:, hh, :], in0=ps_o.rearrange("p a b -> p (a b)"),
                                                        in1=self.rz[:], op=ALU.mult),
                 reads=[("ps", 4), ("ps", 5), "rz"], writes=[("q", hh)])

        load_kv(0)
        QK(0)
        n = len(steps)
        for i in range(n):
            hh, kt = steps[i]
            if kt == 0 and hh % 4 == 0 and hh // 4 + 1 < NKV and hh + 4 < dbg_heads + 3:
                load_kv(hh // 4 + 1)
            if i + 1 < n:
                QK(i + 1)
            EXP(i)
            PV(i)
            if kt == NKT - 1:
                FIN(hh)


    def rnn_pre(self, w_in, g_mix, g_name, xy_out):
        nc, k = self.nc, self.k
        w_v = w_in.rearrange("(k p) n -> p k n", p=128)
        self.stg = [k.sb(f"stg{i}", [128, BLK], F32) for i in range(3)]
        for b in range(NBLK):
            bs = slice(b * BLK, (b + 1) * BLK)
            self.rmsnorm(b, g_mix, g_name)
            ws = 0
            for c in range(2 * KC):
                if c % 4 == 0:
                    ws = self.ring("wq", 2)
                    self.dma("pool", self.wq[ws][:], w_v[:, :, c * 128:c * 128 + 512], writes=[("wq", ws)])
                j = c % 2
                pq = self.psb[j]
                for kc in range(KC):
                    k.op("pe", lambda: nc.tensor.matmul(pq, lhsT=self.wq[ws][:, kc, (c % 4) * 128:(c % 4 + 1) * 128],
                                                        rhs=self.hT[:, kc, :], start=(kc == 0), stop=(kc == KC - 1)),
                         reads=[("wq", ws), ("h", kc)], writes=[("ps", j)], inc=(kc == KC - 1))
                sg = self.ring("stg", 3)
                fn = AF.Copy if c < KC else AF.Gelu_apprx_tanh
                k.op("act", lambda: nc.scalar.activation(out=self.stg[sg][:], in_=pq, func=fn),
                     reads=[("ps", j)], writes=[("stg", sg)])
                self.dma("sp", xy_out[c * 128:(c + 1) * 128, bs], self.stg[sg][:], reads=[("stg", sg)], writes=[("xy_out", c, b)])
        k.finish([("xy_out", c, b) for c in range(2 * KC) for b in range(NBLK)])

    def rnn_core(self, xb_d, yb_d, cw_d, cb_d, wa_d, ba_d, wi_d, bi_d, lam_d, out_d):
        nc, k = self.nc, self.k
        HALF = S // 2
        xbp = k.sb("xbp", [128, 2, S + 4], F32)
        xc = k.sb("xc", [128, 2, S], F32)
        abuf = k.sb("abuf", [128, HALF], F32)
        ubuf = k.sb("ubuf", [128, HALF], F32)
        cw = k.sb("cw", [128, 2, 4], F32)
        cb = k.sb("cb", [128, 2], F32)
        ba = k.sb("ba", [128, 2, 2], F32)
        bi = k.sb("bi", [128, 2, 2], F32)
        lam = k.sb("lam", [128, 2, 2], F32)
        sp1 = k.sb("sp1", [128, 4], F32)
        sp2 = k.sb("sp2", [128, 4], F32)
        tmp4 = [k.sb(f"tmp4_{i}", [128, 4], F32) for i in range(3)]
        wa = k.sb("wa", [128, 2, 2, 256], BF16)
        wi = k.sb("wi", [128, 2, 2, 256], BF16)
        xcb = [k.sb(f"xcb{i}", [128, 2, BLK], BF16) for i in range(2)]
        tr = [k.sb(f"tr{i}", [128, BLK], F32) for i in range(2)]
        ti = [k.sb(f"ti{i}", [128, BLK], F32) for i in range(2)]
        ta = [k.sb(f"ta{i}", [128, BLK], F32) for i in range(2)]
        tq = [k.sb(f"tq{i}", [128, BLK], F32) for i in range(2)]
        ybb = [k.sb(f"ybb{i}", [128, 2, BLK], F32) for i in range(2)]
        hl = k.sb("hl", [128, 1], F32)
        with nc.allow_non_contiguous_dma(reason="tiny parameter vectors"):
            self.dma("sp", cw[:], cw_d, writes=["cw"])
            self.dma("sp", cb[:], cb_d, writes=["cb"])
            self.dma("sp", ba[:], ba_d, writes=["ba"])
            self.dma("sp", bi[:], bi_d, writes=["bi"])
            self.dma("sp", lam[:], lam_d, writes=["lam"])
        for d in range(2):
            self.dma("pool", wa[:, d], wa_d[d].rearrange("(k p) n -> p k n", p=128), writes=[("wa", d)])
            self.dma("pool", wi[:, d], wi_d[d].rearrange("(k p) n -> p k n", p=128), writes=[("wi", d)])
        k.op("dve", lambda: nc.vector.memset(xbp[:, :, 0:2], 0.0), writes=["xpadl"])
        k.op("dve", lambda: nc.vector.memset(xbp[:, :, S + 2:S + 4], 0.0), writes=["xpadr"])
        for m in range(2):
            self.dma("sp", xbp[:, m, 2:S + 2], xb_d[m * 128:(m + 1) * 128, :], writes=[("xbp", m)])
        lamf = lam[:].rearrange("p d m -> p (d m)")
        k.op("act", lambda: nc.scalar.activation(out=tmp4[0][:], in_=lamf, func=AF.Abs),
             reads=["lam"], writes=["t40"])
        k.op("act", lambda: nc.scalar.activation(out=tmp4[1][:], in_=tmp4[0][:], func=AF.Exp, scale=-1.0),
             reads=["t40"], writes=["t41"])
        k.op("act", lambda: nc.scalar.activation(out=tmp4[1][:], in_=tmp4[1][:], func=AF.Ln, bias=1.0),
             reads=["t41"], writes=["t41"])
        k.op("dve", lambda: nc.vector.tensor_scalar(out=tmp4[2][:], in0=lamf, scalar1=-1.0, scalar2=0.0, op0=ALU.mult, op1=ALU.max),
             reads=["lam"], writes=["t42"])
        k.op("dve", lambda: nc.vector.tensor_tensor(out=tmp4[2][:], in0=tmp4[2][:], in1=tmp4[1][:], op=ALU.add),
             reads=["t42", "t41"], writes=["t42"])
        k.op("dve", lambda: nc.vector.tensor_scalar(out=sp1[:], in0=tmp4[2][:], scalar1=-LRU_C, scalar2=None, op0=ALU.mult),
             reads=["t42"], writes=["sp1"])
        k.op("dve", lambda: nc.vector.tensor_scalar(out=sp2[:], in0=tmp4[2][:], scalar1=-2.0 * LRU_C, scalar2=None, op0=ALU.mult),
             reads=["t42"], writes=["sp2"])
        for m in range(2):
            k.op("dve", lambda: nc.vector.tensor_scalar(out=xc[:, m, :], in0=xbp[:, m, 2:S + 2], scalar1=cw[:, m, 2:3], scalar2=cb[:, m:m + 1],
                                                        op0=ALU.mult, op1=ALU.add),
                 reads=[("xbp", m), "cw", "cb", "xpadl", "xpadr"], writes=[("xc", m)])
            for kk in (0, 1, 3):
                k.op("dve", lambda: nc.vector.scalar_tensor_tensor(out=xc[:, m, :], in0=xbp[:, m, kk:kk + S], scalar=cw[:, m, kk:kk + 1],
                                                                   in1=xc[:, m, :], op0=ALU.mult, op1=ALU.add),
                     reads=[("xbp", m), "cw", ("xc", m)], writes=[("xc", m)])
        acc = xbp
        NB = S // BLK
        for d in range(2):
            for m in range(2):
                col = d * 2 + m
                halves = (0, 1) if d == 0 else (1, 0)
                for hi, hf in enumerate(halves):
                    for tbi in range(NB // 2):
                        tb = hf * (NB // 2) + tbi
                        ts_ = slice(tb * BLK, (tb + 1) * BLK)
                        ls = slice(tbi * BLK, (tbi + 1) * BLK)
                        xi = self.ring("xcb", 2)
                        k.op("act", lambda: nc.scalar.copy(out=xcb[xi][:], in_=xc[:, :, ts_]),
                             reads=[("xc", 0), ("xc", 1)], writes=[("xcb", xi)])
                        j = self.ring("rg", 2)
                        pa, pi = self.psb[j], self.psb[2 + j]
                        for kc in range(2):
                            k.op("pe", lambda: nc.tensor.matmul(pa, lhsT=wa[:, d, kc, m * 128:(m + 1) * 128], rhs=xcb[xi][:, kc, :],
                                                                start=(kc == 0), stop=(kc == 1)),
                                 reads=[("wa", d), ("xcb", xi)], writes=[("ps", j)], inc=(kc == 1))
                        for kc in range(2):
                            k.op("pe", lambda: nc.tensor.matmul(pi, lhsT=wi[:, d, kc, m * 128:(m + 1) * 128], rhs=xcb[xi][:, kc, :],
                                                                start=(kc == 0), stop=(kc == 1)),
                                 reads=[("wi", d), ("xcb", xi)], writes=[("ps", 2 + j)], inc=(kc == 1))
                        t = self.ring("rt_", 2)
                        k.op("act", lambda: nc.scalar.activation(out=tr[t][:], in_=pa, func=AF.Sigmoid, bias=ba[:, d, m:m + 1]),
                             reads=[("ps", j), "ba"], writes=[("tr", t)])
                        k.op("act", lambda: nc.scalar.activation(out=ti[t][:], in_=pi, func=AF.Sigmoid, bias=bi[:, d, m:m + 1]),
                             reads=[("ps", 2 + j), "bi"], writes=[("ti", t)])
                        k.op("act", lambda: nc.scalar.activation(out=abuf[:, ls], in_=tr[t][:], func=AF.Exp, scale=sp1[:, col:col + 1]),
                             reads=[("tr", t), "sp1"], writes=[("abuf", tbi)])
                        k.op("act", lambda: nc.scalar.activation(out=ta[t][:], in_=tr[t][:], func=AF.Exp, scale=sp2[:, col:col + 1]),
                             reads=[("tr", t), "sp2"], writes=[("ta", 0)])
                        k.op("dve", lambda: nc.vector.tensor_scalar(out=ta[t][:], in0=ta[t][:], scalar1=-1.0, scalar2=1.0, op0=ALU.mult, op1=ALU.add),
                             reads=[("ta", 0)], writes=[("ta", 0)])
                        k.op("act", lambda: nc.scalar.activation(out=tq[t][:], in_=ta[t][:], func=AF.Sqrt),
                             reads=[("ta", 0)], writes=[("tq", t)])
                        k.op("dve", lambda: nc.vector.tensor_tensor(out=ti[t][:], in0=ti[t][:], in1=xc[:, m, ts_], op=ALU.mult),
                             reads=[("ti", t), ("xc", m)], writes=[("ti", t)])
                        k.op("dve", lambda: nc.vector.tensor_tensor(out=ubuf[:, ls], in0=ti[t][:], in1=tq[t][:], op=ALU.mult),
                             reads=[("ti", t), ("tq", t)], writes=[("ubuf", tbi)])
                    allb = [("abuf", i) for i in range(NB // 2)] + [("ubuf", i) for i in range(NB // 2)]
                    hs = slice(hf * HALF, (hf + 1) * HALF)
                    init = 0.0 if hi == 0 else hl[:, 0:1]
                    if d == 0:
                        k.op("dve", lambda: nc.vector.tensor_tensor_scan(out=acc[:, m, hs], data0=abuf[:], data1=ubuf[:], initial=init,
                                                                         op0=ALU.mult, op1=ALU.add),
                             reads=allb + ["hl", ("xbp", m)], writes=[("xbp", m)])
                        k.op("dve", lambda: nc.vector.tensor_copy(out=hl[:], in_=acc[:, m, (hf + 1) * HALF - 1:(hf + 1) * HALF]),
                             reads=[("xbp", m)], writes=["hl"])
                    else:
                        k.op("dve", lambda: nc.vector.tensor_tensor_scan(out=ubuf[:, ::-1], data0=abuf[:, ::-1], data1=ubuf[:, ::-1], initial=init,
                                                                         op0=ALU.mult, op1=ALU.add),
                             reads=allb + ["hl"], writes=[("ubuf", i) for i in range(NB // 2)])
                        k.op("dve", lambda: nc.vector.tensor_copy(out=hl[:], in_=ubuf[:, 0:1]),
                             reads=[("ubuf", 0)], writes=["hl"])
                        k.op("dve", lambda: nc.vector.tensor_tensor(out=acc[:, m, hs], in0=acc[:, m, hs], in1=ubuf[:], op=ALU.add),
                             reads=[("ubuf", i) for i in range(NB // 2)] + [("xbp", m)], writes=[("xbp", m)])
        yv = yb_d.rearrange("(m p) t -> p m t", p=128)
        ov = out_d.rearrange("(m p) t -> p m t", p=128)
        for tb in range(NB):
            ts_ = slice(tb * BLK, (tb + 1) * BLK)
            yi = self.ring("ybb", 2)
            self.dma("sp", ybb[yi][:], yv[:, :, ts_], writes=[("ybb", yi)])
            k.op("dve", lambda: nc.vector.tensor_tensor(out=ybb[yi][:], in0=ybb[yi][:], in1=acc[:, :, ts_], op=ALU.mult),
                 reads=[("ybb", yi), ("xbp", 0), ("xbp", 1)], writes=[("ybb", yi)])
            self.dma("sp", ov[:, :, ts_], ybb[yi][:], reads=[("ybb", yi)], writes=[("out_d", tb)])
        k.finish([("out_d", tb) for tb in range(NB)])


    def allgather(self, in_t, out_t, reads, writes):
        nc = self.nc
        return self.k.op("pool", lambda: nc.gpsimd.collective_compute(
            "AllGather", ALU.bypass, replica_groups=[list(range(NCORES))],
            ins=[in_t.ap().opt()], outs=[out_t.ap().opt()]), reads=reads, writes=writes, cc=True)

    SP_RNN_KEEP = [("xbp", [128, KC, TOK + 4], F32), ("yg", [128, KC, TOK], BF16)]
    SP_RNN2 = [("edge", [128, KC, 3], F32), ("eall", [128, NCORES, KC * 3], F32), ("carr", [128, 2, KC, 2], F32),
               ("call_", [128, NCORES, 2 * KC * 2], F32), ("sel", [128, 2, KC], F32), ("state", [128, KC], F32),
               ("srs", [128, 2, KC, NBLK], F32), ("sra", [128, 2, KC], F32),
               ("gw", [128, 2, 2, 2, 256], BF16, 2), ("xcb", [128, 2, BLK], BF16, 2),
               ("tr", [128, BLK], F32, 2), ("ti", [128, BLK], F32, 2), ("ta", [128, BLK], F32, 1),
               ("ab", [128, BLK], F32, 2), ("ub", [128, BLK], F32, 2), ("hb", [128, BLK], F32, 2),
               ("acc", [128, TOK], F32, 1),
               ("sp0", [128, 2 * KC], F32), ("sp1", [128, 2 * KC], F32), ("sp2", [128, 2 * KC], F32), ("sp3", [128, 2 * KC], F32)]

    def rnn_tok(self, w_in, g_mix, g_name, cw, cb, ba, bi, lam, wa_d, wi_d, mL, mR, mE, e_in, e_all, c_in, c_all, tag):
        nc, k = self.nc, self.k
        w_v = w_in.rearrange("(k p) n -> p k n", p=128)
        self.phase(self.SP_RNN_KEEP + self.SP_NORM + self.SP_PROJ)
        xbp, yg = self.xbp, self.yg
        for b in range(NBLK):
            bs = slice(b * BLK, (b + 1) * BLK)
            self.rmsnorm(b, g_mix, g_name)
            ws = 0
            for c in range(2 * KC):
                if c % 2 == 0:
                    ws = self.ring("pw", 2)
                    self.dma("pool", self.pw[ws][:], w_v[:, :, c * 128:c * 128 + 256], writes=[("pw", ws)])
                j = c % 2
                pq = self.psb[j]
                for kc in range(KC):
                    k.op("pe", lambda: nc.tensor.matmul(pq, lhsT=self.pw[ws][:, kc, (c % 2) * 128:(c % 2 + 1) * 128],
                                                        rhs=self.hT[:, kc, :], start=(kc == 0), stop=(kc == KC - 1)),
                         reads=[("pw", ws), ("h", kc)], writes=[("ps", j)], inc=(kc == KC - 1))
                if c < KC:
                    k.op("act", lambda: nc.scalar.copy(out=xbp[:, c, 2 + b * BLK:2 + (b + 1) * BLK], in_=pq),
                         reads=[("ps", j)], writes=[("xb", c, b)])
                else:
                    k.op("act", lambda: nc.scalar.activation(out=yg[:, c - KC, bs], in_=pq, func=AF.Gelu_apprx_tanh),
                         reads=[("ps", j)], writes=[("yg", c - KC, b)])
        self.phase(self.SP_RNN_KEEP + self.SP_RNN2)
        xbp, yg = self.xbp, self.yg
        self.ctmp = self.acc[0]
        allxb = [("xb", c, b) for c in range(KC) for b in range(NBLK)]
        k.op("dve", lambda: nc.vector.tensor_copy(out=self.edge[:, :, 0:1], in_=xbp[:, :, 2:3]), reads=allxb, writes=["edge0"])
        k.op("dve", lambda: nc.vector.tensor_copy(out=self.edge[:, :, 1:3], in_=xbp[:, :, TOK:TOK + 2]), reads=allxb, writes=["edge1"])
        self.dma("sp", e_in.ap(), self.edge[:].rearrange("p k e -> p (k e)"), reads=["edge0", "edge1"], writes=[("e_in", tag)])
        self.allgather(e_in, e_all, reads=[("e_in", tag)], writes=[("e_all", tag)])
        self.dma("sp", self.eall[:], e_all.ap().rearrange("(r p) f -> p r f", p=128), reads=[("e_all", tag)], writes=["eall"])
        ev = self.eall[:].rearrange("p r (k e) -> p r k e", e=3)
        k.op("dve", lambda: nc.vector.memset(xbp[:, :, 0:2], 0.0), writes=["hl"])
        k.op("dve", lambda: nc.vector.memset(xbp[:, :, TOK + 2:TOK + 4], 0.0), writes=["hr"])
        for r in range(NCORES):
            k.op("dve", lambda: nc.vector.scalar_tensor_tensor(out=xbp[:, :, 0:2], in0=ev[:, r, :, 1:3], scalar=mL[:, r:r + 1],
                                                               in1=xbp[:, :, 0:2], op0=ALU.mult, op1=ALU.add),
                 reads=["eall", "mL", "hl"], writes=["hl"])
            k.op("dve", lambda: nc.vector.scalar_tensor_tensor(out=xbp[:, :, TOK + 2:TOK + 3], in0=ev[:, r, :, 0:1], scalar=mR[:, r:r + 1],
                                                               in1=xbp[:, :, TOK + 2:TOK + 3], op0=ALU.mult, op1=ALU.add),
                 reads=["eall", "mR", "hr"], writes=["hr"])
        for c in range(KC):
            xr = [("xb", c, b) for b in range(NBLK)]
            k.op("dve", lambda: nc.vector.tensor_scalar(out=self.ctmp[:], in0=xbp[:, c, 2:TOK + 2], scalar1=cw[:, c, 2:3], scalar2=cb[:, c:c + 1],
                                                        op0=ALU.mult, op1=ALU.add),
                 reads=xr + ["cw", "cb", "hl", "hr"], writes=[("acc", 0, 0), ("acc", 0, 1)])
            for kk in (0, 1, 3):
                k.op("dve", lambda: nc.vector.scalar_tensor_tensor(out=self.ctmp[:], in0=xbp[:, c, kk:kk + TOK], scalar=cw[:, c, kk:kk + 1],
                                                                   in1=self.ctmp[:], op0=ALU.mult, op1=ALU.add),
                     reads=xr + ["cw", ("acc", 0, 0), ("acc", 0, 1)], writes=[("acc", 0, 0), ("acc", 0, 1)])
            k.op("act", lambda: nc.scalar.copy(out=xbp[:, c, 2:TOK + 2], in_=self.ctmp[:]), reads=[("acc", 0, 0), ("acc", 0, 1)], writes=xr)
        lamf = lam[:].rearrange("p d k -> p (d k)")
        k.op("act", lambda: nc.scalar.activation(out=self.sp0[:], in_=lamf, func=AF.Abs), reads=["lam"], writes=["sp0"])
        k.op("act", lambda: nc.scalar.activation(out=self.sp0[:], in_=self.sp0[:], func=AF.Exp, scale=-1.0), reads=["sp0"], writes=["sp0"])
        k.op("act", lambda: nc.scalar.activation(out=self.sp0[:], in_=self.sp0[:], func=AF.Ln, bias=1.0), reads=["sp0"], writes=["sp0"])
        k.op("dve", lambda: nc.vector.tensor_scalar(out=self.sp3[:], in0=lamf, scalar1=-1.0, scalar2=0.0, op0=ALU.mult, op1=ALU.max),
             reads=["lam"], writes=["sp3"])
        k.op("dve", lambda: nc.vector.tensor_tensor(out=self.sp3[:], in0=self.sp3[:], in1=self.sp0[:], op=ALU.add), reads=["sp3", "sp0"], writes=["sp3"])
        k.op("dve", lambda: nc.vector.tensor_scalar(out=self.sp1[:], in0=self.sp3[:], scalar1=-LRU_C, scalar2=None, op0=ALU.mult), reads=["sp3"], writes=["sp1"])
        k.op("dve", lambda: nc.vector.tensor_scalar(out=self.sp2[:], in0=self.sp3[:], scalar1=-2.0 * LRU_C, scalar2=None, op0=ALU.mult), reads=["sp3"], writes=["sp2"])
        wav = wa_d.rearrange("d n (k p) m -> p d n k m", p=128)
        wiv = wi_d.rearrange("d n (k p) m -> p d n k m", p=128)

        def load_gw(n):
            s_ = self.ring("gw", 2)
            for d in range(2):
                self.dma("pool", self.gw[s_][:, d, 0], wav[:, d, n], writes=[("gw", s_, d, 0)])
                self.dma("pool", self.gw[s_][:, d, 1], wiv[:, d, n], writes=[("gw", s_, d, 1)])
            return s_

        def unit(d, c, b, gs, first_pass):
            n, m = c // 2, c % 2
            bs = slice(b * BLK, (b + 1) * BLK)
            col = d * KC + c
            xi = self.ring("xcb", 2)
            k.op("pool", lambda: nc.gpsimd.tensor_copy(out=self.xcb[xi][:], in_=xbp[:, 2 * n:2 * n + 2, 2 + b * BLK:2 + (b + 1) * BLK]),
                 reads=[("xb", 2 * n, b), ("xb", 2 * n + 1, b)], writes=[("xcb", xi)])
            j = self.ring("rg", 2)
            pa, pi = self.psb[j], self.psb[2 + j]
            for kc in range(2):
                k.op("pe", lambda: nc.tensor.matmul(pa, lhsT=self.gw[gs][:, d, 0, kc, m * 128:(m + 1) * 128], rhs=self.xcb[xi][:, kc, :],
                                                    start=(kc == 0), stop=(kc == 1)),
                     reads=[("gw", gs, d, 0), ("xcb", xi)], writes=[("ps", j)], inc=(kc == 1))
            for kc in range(2):
                k.op("pe", lambda: nc.tensor.matmul(pi, lhsT=self.gw[gs][:, d, 1, kc, m * 128:(m + 1) * 128], rhs=self.xcb[xi][:, kc, :],
                                                    start=(kc == 0), stop=(kc == 1)),
                     reads=[("gw", gs, d, 1), ("xcb", xi)], writes=[("ps", 2 + j)], inc=(kc == 1))
            t = self.ring("rt_", 2)
            if first_pass:
                k.op("act", lambda: nc.scalar.activation(out=self.tr[t][:], in_=pa, func=AF.Sigmoid, bias=ba[:, d, c:c + 1],
                                                         accum_out=self.srs[:, d, c, b:b + 1]),
                     reads=[("ps", j), "ba"], writes=[("tr", t), ("srs", d, c, b)])
            else:
                k.op("act", lambda: nc.scalar.activation(out=self.tr[t][:], in_=pa, func=AF.Sigmoid, bias=ba[:, d, c:c + 1]),
                     reads=[("ps", j), "ba"], writes=[("tr", t)])
            k.op("act", lambda: nc.scalar.activation(out=self.ti[t][:], in_=pi, func=AF.Sigmoid, bias=bi[:, d, c:c + 1]),
                 reads=[("ps", 2 + j), "bi"], writes=[("ti", t)])
            k.op("act", lambda: nc.scalar.activation(out=self.ab[t][:], in_=self.tr[t][:], func=AF.Exp, scale=self.sp1[:, col:col + 1]),
                 reads=[("tr", t), "sp1"], writes=[("ab", t)])
            k.op("act", lambda: nc.scalar.activation(out=self.ta[0][:], in_=self.tr[t][:], func=AF.Exp, scale=self.sp2[:, col:col + 1]),
                 reads=[("tr", t), "sp2"], writes=[("ta", 0)])
            k.op("act", lambda: nc.scalar.activation(out=self.ta[0][:], in_=self.ta[0][:], func=AF.Sqrt, scale=-1.0, bias=1.0),
                 reads=[("ta", 0)], writes=[("ta", 0)])
            k.op("dve", lambda: nc.vector.tensor_tensor(out=self.ti[t][:], in0=self.ti[t][:], in1=xbp[:, c, 2 + b * BLK:2 + (b + 1) * BLK], op=ALU.mult),
                 reads=[("ti", t), ("xb", c, b)], writes=[("ti", t)])
            k.op("dve", lambda: nc.vector.tensor_tensor(out=self.ub[t][:], in0=self.ti[t][:], in1=self.ta[0][:], op=ALU.mult),
                 reads=[("ti", t), ("ta", 0)], writes=[("ub", t)])
            return t

        def scan(d, t, out_ap, init, reads, writes):
            if d == 0:
                k.op("dve", lambda: nc.vector.tensor_tensor_scan(out=out_ap, data0=self.ab[t][:], data1=self.ub[t][:], initial=init,
                                                                 op0=ALU.mult, op1=ALU.add),
                     reads=[("ab", t), ("ub", t)] + reads, writes=writes)
            else:
                k.op("dve", lambda: nc.vector.tensor_tensor_scan(out=out_ap[:, ::-1], data0=self.ab[t][:, ::-1], data1=self.ub[t][:, ::-1],
                                                                 initial=init, op0=ALU.mult, op1=ALU.add),
                     reads=[("ab", t), ("ub", t)] + reads, writes=writes)

        gs = 0
        for c in range(KC):
            if c % 2 == 0:
                gs = load_gw(c // 2)
            for d in range(2):
                prev = None
                for b in ((0, 1) if d == 0 else (1, 0)):
                    t = unit(d, c, b, gs, True)
                    h = self.ring("hb", 2)
                    if prev is None:
                        scan(d, t, self.hb[h][:], 0.0, [], [("hb", h)])
                    else:
                        pcol = BLK - 1 if d == 0 else 0
                        scan(d, t, self.hb[h][:], self.hb[prev][:, pcol:pcol + 1], [("hb", prev)], [("hb", h)])
                    prev = h
                pcol = BLK - 1 if d == 0 else 0
                k.op("dve", lambda: nc.vector.tensor_copy(out=self.carr[:, d, c, 1:2], in_=self.hb[prev][:, pcol:pcol + 1]),
                     reads=[("hb", prev)], writes=[("carrH", d, c)])
        allsr = [("srs", d, c, b) for d in range(2) for c in range(KC) for b in range(NBLK)]
        k.op("dve", lambda: nc.vector.tensor_tensor(out=self.sra[:], in0=self.srs[:, :, :, 0], in1=self.srs[:, :, :, 1], op=ALU.add),
             reads=allsr, writes=["sra"])
        k.op("dve", lambda: nc.vector.tensor_tensor(out=self.sra[:], in0=self.sra[:], in1=self.sp1[:].rearrange("p (d k) -> p d k", d=2), op=ALU.mult),
             reads=["sra", "sp1"], writes=["sra"])
        k.op("act", lambda: nc.scalar.activation(out=self.carr[:, :, :, 0], in_=self.sra[:], func=AF.Exp), reads=["sra"], writes=["carrA"])
        self.dma("sp", c_in.ap(), self.carr[:].rearrange("p d k e -> p (d k e)"),
                 reads=["carrA"] + [("carrH", d, c) for d in range(2) for c in range(KC)], writes=[("c_in", tag)])
        self.allgather(c_in, c_all, reads=[("c_in", tag)], writes=[("c_all", tag)])
        self.dma("sp", self.call_[:], c_all.ap().rearrange("(r p) f -> p r f", p=128), reads=[("c_all", tag)], writes=["call"])
        cv = self.call_[:].rearrange("p r (d k e) -> p r d k e", d=2, e=2)
        for d in range(2):
            k.op("dve", lambda: nc.vector.memset(self.state[:], 0.0), writes=["state"])
            k.op("dve", lambda: nc.vector.memset(self.sel[:, d, :], 0.0), writes=[("sel", d)])
            for r in (range(NCORES) if d == 0 else range(NCORES - 1, -1, -1)):
                k.op("dve", lambda: nc.vector.scalar_tensor_tensor(out=self.sel[:, d, :], in0=self.state[:], scalar=mE[:, r:r + 1],
                                                                   in1=self.sel[:, d, :], op0=ALU.mult, op1=ALU.add),
                     reads=["state", "mE", ("sel", d)], writes=[("sel", d)])
                k.op("dve", lambda: nc.vector.tensor_tensor(out=self.state[:], in0=self.state[:], in1=cv[:, r, d, :, 0], op=ALU.mult),
                     reads=["state", "call"], writes=["state"])
                k.op("dve", lambda: nc.vector.tensor_tensor(out=self.state[:], in0=self.state[:], in1=cv[:, r, d, :, 1], op=ALU.add),
                     reads=["state", "call"], writes=["state"])
        for c in range(KC):
            if c % 2 == 0:
                gs = load_gw(c // 2)
            ai = 0
            acc = self.acc[ai]
            for d in range(2):
                prev = None
                for b in ((0, 1) if d == 0 else (1, 0)):
                    bs = slice(b * BLK, (b + 1) * BLK)
                    t = unit(d, c, b, gs, False)
                    if d == 0:
                        if prev is None:
                            scan(d, t, acc[:, bs], self.sel[:, d, c:c + 1], [("sel", d)], [("acc", ai, b)])
                        else:
                            scan(d, t, acc[:, bs], acc[:, BLK - 1:BLK], [("acc", ai, 0)], [("acc", ai, b)])
                        prev = b
                    else:
                        h = self.ring("hb", 2)
                        if prev is None:
                            scan(d, t, self.hb[h][:], self.sel[:, d, c:c + 1], [("sel", d)], [("hb", h)])
                        else:
                            scan(d, t, self.hb[h][:], self.hb[prev][:, 0:1], [("hb", prev)], [("hb", h)])
                        prev = h
                        k.op("dve", lambda: nc.vector.tensor_tensor(out=acc[:, bs], in0=acc[:, bs], in1=self.hb[h][:], op=ALU.add),
                             reads=[("acc", ai, b), ("hb", h)], writes=[("acc", ai, b)])
                        k.op("dve", lambda: nc.vector.tensor_tensor(out=yg[:, c, bs], in0=acc[:, bs], in1=yg[:, c, bs], op=ALU.mult),
                             reads=[("acc", ai, b), ("yg", c, b)], writes=[("yg", c, b)])


ARENA_BYTES = 142848


def _pk(v):
    v = np.asarray(v, np.float32)
    lead = v.shape[:-1]
    return np.ascontiguousarray(np.moveaxis(v.reshape(*lead, KC, 128), -1, 0))


def core_masks(core):
    def oh(i):
        m = np.zeros((128, NCORES), np.float32)
        if 0 <= i < NCORES:
            m[:, i] = 1.0
        return m
    return {"mL": oh(core - 1), "mR": oh(core + 1), "mE": oh(core)}


def _rope_tables():
    f = (10000.0 ** (-np.arange(32, dtype=np.float32) / 32)).astype(np.float32)
    t = np.arange(S)
    row = (t // GRID_W).astype(np.float32)[:, None] * f
    col = (t % GRID_W).astype(np.float32)[:, None] * f
    cos = np.concatenate([np.cos(row), np.cos(row), np.cos(col), np.cos(col)], 1).astype(np.float32)
    sin = np.concatenate([-np.sin(row), np.sin(row), -np.sin(col), np.sin(col)], 1).astype(np.float32)
    rm = np.zeros((128, 128), np.float32)
    for dd in range(128):
        p = dd + 32 if (dd % 64) < 32 else dd - 32
        rm[p, dd] = 1.0
    return cos, sin, rm


def build_fused(layers=(0, 1, 2, 3), do_ffn=True):
    nc = bass.Bass("TRN2", target_bir_lowering=False)
    din = lambda name, shape, dt_=F32: nc.dram_tensor(name, shape, dt_, kind="ExternalInput").ap()
    x = din("xT", [D, TOK])
    y = nc.dram_tensor("yT", [D, TOK], F32, kind="ExternalOutput").ap()
    P = Prog(nc)
    k = P.k
    P.alloc_x()
    P.make_arena(ARENA_BYTES)
    P.load_x(x)
    cos_d, sin_d, rm_d = din("cosT", [128, TOK]), din("sinT", [128, TOK]), din("c_rm", [128, 128])

    def small(name, shape):
        t = k.sb(name, shape, F32)
        P.dma("sp", t[:], din(name, shape), writes=[name])
        return t

    gm = small("g_mix", [128, 4, KC])
    gf = small("g_ffn", [128, 4, KC])
    mL, mR, mE = small("mL", [128, NCORES]), small("mR", [128, NCORES]), small("mE", [128, NCORES])
    for i in layers:
        j = i // 2
        sfx = str(j)
        if i % 2 == 0:
            wqkv, wo = din("wqkv" + sfx, [D, QKV]), din("wo" + sfx, [D, D])
            qg, kg = small("qg" + sfx, [128, 1]), small("kg" + sfx, [128, 1])
            k_loc = nc.dram_tensor("k_loc" + sfx, [NKV * 128, TOK], BF16)
            k_all = nc.dram_tensor("k_all" + sfx, [NCORES * NKV * 128, TOK], BF16)
            v_loc = nc.dram_tensor("v_loc" + sfx, [TOK, NKV * HD], BF16)
            v_all = nc.dram_tensor("v_all" + sfx, [NCORES * TOK, NKV * HD], BF16)
            P.phase(P.SP_QT + P.SP_NORM + P.SP_ATTN_PRE)
            P.attn_pre(wqkv, gm[:, i, :], "g_mix", qg, kg, k_loc.ap().rearrange("(g p) t -> g p t", p=128), v_loc.ap(),
                       cos_d, sin_d, rm_d)
            P.allgather(k_loc, k_all, reads=[("kT_out", g) for g in range(NKV)], writes=["k_all"])
            P.allgather(v_loc, v_all, reads=["v_out"], writes=["v_all"])
            P.phase(P.SP_QT + P.SP_ATTN_CORE + P.SP_PROJ)
            P.attn_core(k_all.ap().rearrange("(r g p) t -> r g p t", r=NCORES, g=NKV),
                        v_all.ap().rearrange("(r t) n -> r t n", r=NCORES))
            P.proj_resid(P.qT, lambda kc: [("q", kc)], wo)
        else:
            w_in, w_out = din("w_in" + sfx, [D, 2 * D]), din("w_out" + sfx, [D, D])
            cw, cb = small("cw" + sfx, [128, KC, 4]), small("cb" + sfx, [128, KC])
            ba, bi, lam = small("ba" + sfx, [128, 2, KC]), small("bi" + sfx, [128, 2, KC]), small("lam" + sfx, [128, 2, KC])
            wa_d, wi_d = din("wa" + sfx, [2, 8, 256, 256]), din("wi" + sfx, [2, 8, 256, 256])
            e_in = nc.dram_tensor("e_in" + sfx, [128, KC * 3], F32)
            e_all = nc.dram_tensor("e_all" + sfx, [NCORES * 128, KC * 3], F32)
            c_in = nc.dram_tensor("c_in" + sfx, [128, 2 * KC * 2], F32)
            c_all = nc.dram_tensor("c_all" + sfx, [NCORES * 128, 2 * KC * 2], F32)
            P.rnn_tok(w_in, gm[:, i, :], "g_mix", cw, cb, ba, bi, lam, wa_d, wi_d, mL, mR, mE, e_in, e_all, c_in, c_all, tag=i)
            P.phase(P.SP_RNN_KEEP + P.SP_PROJ)
            P.proj_resid(P.yg, lambda kc: [("yg", kc, b) for b in range(NBLK)], w_out)
        if not do_ffn:
            continue
        wg, wu, wd = din("wg" + str(i), [D, DFF]), din("wu" + str(i), [D, DFF]), din("wd" + str(i), [DFF, D])
        P.phase(P.SP_NORM + P.SP_WRING + P.SP_FFN)
        for b in range(NBLK):
            P.rmsnorm(b, gf[:, i, :], "g_ffn")
            P.ffn(b, wg, wu, wd)
    P.store_x(y)
    P.stats = (k.n_ins, k.n_wait)
    k.close()
    return nc


def host_inputs(core, x, norm_mix, norm_ffn, attn_w_qkv, attn_q_gain, attn_k_gain, attn_w_o,
                rnn_w_in, rnn_conv_w, rnn_conv_b, rnn_w_a, rnn_b_a, rnn_w_i, rnn_b_i,
                rnn_lambda, rnn_w_out, ffn_w_gate, ffn_w_up, ffn_w_down, tables, layers=(0, 1, 2, 3), shared=None):
    cos, sin, rm = tables
    f32 = lambda a: np.ascontiguousarray(np.asarray(a, dtype=np.float32))
    sl = slice(core * TOK, (core + 1) * TOK)
    d = {"xT": np.ascontiguousarray(x[sl].T), "cosT": np.ascontiguousarray(cos[sl].T), "sinT": np.ascontiguousarray(sin[sl].T)}
    d.update(core_masks(core))
    if shared is not None and shared:
        d.update(shared)
        return d
    sh = {"c_rm": rm, "g_mix": np.ascontiguousarray(_pk(norm_mix)), "g_ffn": np.ascontiguousarray(_pk(norm_ffn))}
    for i in layers:
        j = i // 2
        sfx = str(j)
        if i % 2 == 0:
            sh["wqkv" + sfx] = f32(attn_w_qkv[j]); sh["wo" + sfx] = f32(attn_w_o[j])
            sh["qg" + sfx] = f32(attn_q_gain[j]).reshape(128, 1); sh["kg" + sfx] = f32(attn_k_gain[j]).reshape(128, 1)
        else:
            sh["w_in" + sfx] = f32(rnn_w_in[j]); sh["w_out" + sfx] = f32(rnn_w_out[j])
            sh["cw" + sfx] = np.ascontiguousarray(_pk(rnn_conv_w[j]).transpose(0, 2, 1))
            sh["cb" + sfx] = _pk(rnn_conv_b[j]); sh["ba" + sfx] = _pk(rnn_b_a[j]); sh["bi" + sfx] = _pk(rnn_b_i[j])
            sh["lam" + sfx] = _pk(rnn_lambda[j])
            sh["wa" + sfx] = f32(rnn_w_a[j]); sh["wi" + sfx] = f32(rnn_w_i[j])
        sh["wg" + str(i)] = f32(ffn_w_gate[i]); sh["wu" + str(i)] = f32(ffn_w_up[i]); sh["wd" + str(i)] = f32(ffn_w_down[i])
    if shared is not None:
        shared.update(sh)
    d.update(sh)
    return d


def kernel(x, norm_mix, norm_ffn, attn_w_qkv, attn_q_gain, attn_k_gain, attn_w_o,
           rnn_w_in, rnn_conv_w, rnn_conv_b, rnn_w_a, rnn_b_a, rnn_w_i, rnn_b_i,
           rnn_lambda, rnn_w_out, ffn_w_gate, ffn_w_up, ffn_w_down):
    x2 = np.ascontiguousarray(np.asarray(x, dtype=np.float32))[0]
    tables = _rope_tables()
    nc = build_fused()
    shared = {}
    ins = [host_inputs(c, x2, norm_mix, norm_ffn, attn_w_qkv, attn_q_gain, attn_k_gain, attn_w_o,
                       rnn_w_in, rnn_conv_w, rnn_conv_b, rnn_w_a, rnn_b_a, rnn_w_i, rnn_b_i,
                       rnn_lambda, rnn_w_out, ffn_w_gate, ffn_w_up, ffn_w_down, tables, shared=shared)
           for c in range(NCORES)]
    res = run_bass_kernel_spmd(nc, ins, core_ids=list(range(NCORES)))
    out = np.empty((1, S, D), np.float32)
    for c in range(NCORES):
        out[0, c * TOK:(c + 1) * TOK, :] = np.asarray(res.results[c]["yT"], np.float32).T
    return out
```
